# Optimizing a Trainium2 kernel written in Bass

```python
import math, functools
import jax, jax.numpy as jnp
from jax import lax
import numpy as np

D_MODEL = 1024
BATCH = 32
SEQ = 2048
DEPTH = 2
DEC_BATCH = 8
DEC_SEQ = 2048
PAST_LEN = 128

N_MIXERS = 2
N_FOURIER_LAYERS = (DEPTH + 1) // 2
N_MLA_LAYERS = DEPTH // 2
N_FOURIER_GROUPS = 4
FOURIER_GROUP_DIM = D_MODEL // N_FOURIER_GROUPS
N_HEADS = 16
QK_NOPE_DIM = 64
QK_ROPE_DIM = 32
V_HEAD_DIM = 64
Q_LORA_RANK = D_MODEL // 4
KV_LORA_RANK = D_MODEL // 8
MLA_IN_DIM = Q_LORA_RANK + KV_LORA_RANK + QK_ROPE_DIM
ATTN_SCALE = (QK_NOPE_DIM + QK_ROPE_DIM) ** -0.5
ROPE_THETA = 10000.0
Q_BLOCK = 128
D_FF = ((8 * D_MODEL // 3 + 127) // 128) * 128
CONV_WIDTH = 3
EPS = 1e-6

kernel_name = "hybrid_fnet_mla_convffn_encoder"


def _rmsnorm(x, g):
    xf = x.astype(jnp.float32)
    y = xf * lax.rsqrt(jnp.mean(xf * xf, axis=-1, keepdims=True) + EPS)
    return (y * g.astype(jnp.float32)).astype(x.dtype)


def _fourier_mix(h, w_out):
    B, S, D = h.shape
    hg = h.astype(jnp.float32).reshape(B, S, N_FOURIER_GROUPS, FOURIER_GROUP_DIM)
    f = jnp.fft.fftn(hg, axes=(1, 3), norm="ortho").real
    return f.reshape(B, S, D).astype(h.dtype) @ w_out


def _rope_tables(S):
    inv = 1.0 / (ROPE_THETA ** (jnp.arange(0, QK_ROPE_DIM, 2, dtype=jnp.float32) / QK_ROPE_DIM))
    ang = jnp.arange(S, dtype=jnp.float32)[:, None] * inv[None, :]
    return jnp.cos(ang), jnp.sin(ang)


def _apply_rope(x, cos, sin):
    xf = x.astype(jnp.float32)
    half = QK_ROPE_DIM // 2
    x1, x2 = xf[..., :half], xf[..., half:]
    return jnp.concatenate([x1 * cos - x2 * sin, x2 * cos + x1 * sin], axis=-1).astype(x.dtype)


def _mla(h, w_in, g_q, g_kv, w_uq, w_ukv, w_o, cos, sin):
    B, S, _ = h.shape
    a = h @ w_in
    c_q = _rmsnorm(a[..., :Q_LORA_RANK], g_q)
    c_kv = _rmsnorm(a[..., Q_LORA_RANK:Q_LORA_RANK + KV_LORA_RANK], g_kv)
    k_rope = _apply_rope(a[..., Q_LORA_RANK + KV_LORA_RANK:], cos, sin)
    q = (c_q @ w_uq).reshape(B, S, N_HEADS, QK_NOPE_DIM + QK_ROPE_DIM)
    q_nope = q[..., :QK_NOPE_DIM]
    q_rope = _apply_rope(q[..., QK_NOPE_DIM:], cos[:, None, :], sin[:, None, :])
    kv = (c_kv @ w_ukv).reshape(B, S, N_HEADS, QK_NOPE_DIM + V_HEAD_DIM)
    k_nope, v = kv[..., :QK_NOPE_DIM], kv[..., QK_NOPE_DIM:]

    nb = S // Q_BLOCK

    def to_blocks(t):
        return jnp.moveaxis(t.reshape(B, nb, Q_BLOCK, *t.shape[2:]), 1, 0)

    def attend(blk):
        qn, qr = blk
        s = (jnp.einsum('bqhd,bkhd->bhqk', qn, k_nope, preferred_element_type=jnp.float32)
             + jnp.einsum('bqhd,bkd->bhqk', qr, k_rope, preferred_element_type=jnp.float32)) * ATTN_SCALE
        p = jax.nn.softmax(s, axis=-1).astype(v.dtype)
        return jnp.einsum('bhqk,bkhd->bqhd', p, v)

    o = lax.map(attend, (to_blocks(q_nope), to_blocks(q_rope)))
    o = jnp.moveaxis(o, 0, 1).reshape(B, S, N_HEADS * V_HEAD_DIM)
    return o @ w_o


def _conv_ffn(h, w_up, conv_w, conv_b, w_down):
    S = h.shape[1]
    u = h @ w_up
    pad = CONV_WIDTH // 2
    up = jnp.pad(u, ((0, 0), (pad, pad), (0, 0)))
    u = sum(up[:, k:k + S] * conv_w[k] for k in range(CONV_WIDTH)) + conv_b
    gate, val = u[..., :D_FF], u[..., D_FF:]
    return (jax.nn.silu(gate) * val) @ w_down


def _trunk(x, norm_mix, w_fourier_out, w_mla_in, g_mla_q, g_mla_kv, w_mla_uq, w_mla_ukv, w_mla_o,
           norm_ffn, w_ffn_up, conv_w, conv_b, w_ffn_down, norm_final):
    cos, sin = _rope_tables(x.shape[1])
    for i in range(DEPTH):
        h = _rmsnorm(x, norm_mix[i])
        j = i // N_MIXERS
        if i % N_MIXERS == 0:
            x = x + _fourier_mix(h, w_fourier_out[j])
        else:
            x = x + _mla(h, w_mla_in[j], g_mla_q[j], g_mla_kv[j], w_mla_uq[j], w_mla_ukv[j],
                         w_mla_o[j], cos, sin)
        h = _rmsnorm(x, norm_ffn[i])
        x = x + _conv_ffn(h, w_ffn_up[i], conv_w[i], conv_b[i], w_ffn_down[i])
    return _rmsnorm(x, norm_final)


def setup_inputs(seed: int = 0) -> dict:
    key = jax.random.key(seed)
    ks = jax.random.split(key, 18)
    f32 = jnp.float32

    def w(k, shape, fan_in):
        return jax.random.normal(k, shape, f32) * (fan_in ** -0.5)

    def gain(k, shape):
        return 1.0 + 0.01 * jax.random.normal(k, shape, f32)

    return {
        "x_prompt": jax.random.normal(ks[0], (BATCH, SEQ, D_MODEL), f32),
        "x_sample": jax.random.normal(ks[1], (DEC_BATCH, DEC_SEQ, D_MODEL), f32),
        "norm_mix": gain(ks[2], (DEPTH, D_MODEL)),
        "w_fourier_out": w(ks[3], (N_FOURIER_LAYERS, D_MODEL, D_MODEL), D_MODEL),
        "w_mla_in": w(ks[4], (N_MLA_LAYERS, D_MODEL, MLA_IN_DIM), D_MODEL),
        "g_mla_q": gain(ks[5], (N_MLA_LAYERS, Q_LORA_RANK)),
        "g_mla_kv": gain(ks[6], (N_MLA_LAYERS, KV_LORA_RANK)),
        "w_mla_uq": w(ks[7], (N_MLA_LAYERS, Q_LORA_RANK, N_HEADS * (QK_NOPE_DIM + QK_ROPE_DIM)), Q_LORA_RANK),
        "w_mla_ukv": w(ks[8], (N_MLA_LAYERS, KV_LORA_RANK, N_HEADS * (QK_NOPE_DIM + V_HEAD_DIM)), KV_LORA_RANK),
        "w_mla_o": w(ks[9], (N_MLA_LAYERS, N_HEADS * V_HEAD_DIM, D_MODEL), N_HEADS * V_HEAD_DIM),
        "norm_ffn": gain(ks[10], (DEPTH, D_MODEL)),
        "w_ffn_up": w(ks[11], (DEPTH, D_MODEL, 2 * D_FF), D_MODEL),
        "conv_w": w(ks[12], (DEPTH, CONV_WIDTH, 2 * D_FF), CONV_WIDTH),
        "conv_b": 0.01 * jax.random.normal(ks[13], (DEPTH, 2 * D_FF), f32),
        "w_ffn_down": w(ks[14], (DEPTH, D_FF, D_MODEL), D_FF),
        "norm_final": gain(ks[15], (D_MODEL,)),
    }


def reference(x_prompt, x_sample, norm_mix, w_fourier_out, w_mla_in, g_mla_q, g_mla_kv, w_mla_uq,
              w_mla_ukv, w_mla_o, norm_ffn, w_ffn_up, conv_w, conv_b, w_ffn_down, norm_final):
    y_prompt = _trunk(x_prompt, norm_mix, w_fourier_out, w_mla_in, g_mla_q, g_mla_kv, w_mla_uq,
                      w_mla_ukv, w_mla_o, norm_ffn, w_ffn_up, conv_w, conv_b, w_ffn_down, norm_final)
    y_sample = _trunk(x_sample, norm_mix, w_fourier_out, w_mla_in, g_mla_q, g_mla_kv, w_mla_uq,
                      w_mla_ukv, w_mla_o, norm_ffn, w_ffn_up, conv_w, conv_b, w_ffn_down, norm_final)
    return (y_prompt, y_sample)
```

```python
import numpy as np
import ml_dtypes
import concourse.bass as bass
import concourse.mybir as mybir
from concourse.bass_utils import run_bass_kernel_spmd

F32 = mybir.dt.float32
BF16 = mybir.dt.bfloat16
AF = mybir.ActivationFunctionType
ALU = mybir.AluOpType

S = 2048
D = 1024
NT = 16
KC = 8
FF = 2816
FC = 22
NH = 16
EPS = 1e-6
ATTN_SCALE = 96.0 ** -0.5
NCORES = 8
NSEQ = 5
NCST = 1412
CST_CONV = 36
CST_GF = 388
PARTS = [list(range(0, 5)), list(range(5, 10)), list(range(10, 14)), list(range(14, 18)), list(range(18, 22))]
NDMASEM = 32
NPOOLSEM = 8
SWAP = list(range(16, 32)) + list(range(0, 16))


class Buf:
    __slots__ = ("name", "w", "r", "rd")

    def __init__(self, name=""):
        self.name = name
        self.w = None
        self.r = {}
        self.rd = []


class Op:
    __slots__ = ("eng", "fn", "deps", "sig", "cnt", "is_dma", "sem", "val", "seq")


class Prog:
    ENGS = ["pe", "act", "dve", "pool", "sp"]

    def __init__(self):
        self.ops = {e: [] for e in self.ENGS}
        self.nseq = 0
        self.ndma = {"sp": 0, "pool": 0}
        self.dma_count = [0] * NDMASEM
        self.dma_last = {}

    def op(self, eng, fn, reads=(), writes=(), extra=()):
        o = Op()
        o.eng = eng
        o.fn = fn
        o.sig = False
        o.is_dma = False
        o.cnt = 0
        o.seq = self.nseq
        self.nseq += 1
        deps = []
        seen = set()

        def add(d):
            if d is None or id(d) in seen:
                return
            seen.add(id(d))
            deps.append(d)

        for b in reads:
            add(b.w)
        for b in writes:
            add(b.w)
            for r in b.r.values():
                add(r)
            for r in b.rd:
                add(r)
        for d in extra:
            add(d)
        o.deps = [d for d in deps if d.is_dma or not (d.eng == "pe" and eng == "pe")]
        for d in o.deps:
            d.sig = True
        self.ops[eng].append(o)
        return o

    def commit(self, o, reads, writes):
        for b in reads:
            if o.is_dma:
                b.rd.append(o)
            else:
                b.r[o.eng] = o
        for b in writes:
            b.w = o
            b.r = {}
            b.rd = []

    def comp(self, eng, fn, reads=(), writes=(), extra=()):
        o = self.op(eng, fn, reads, writes, extra)
        self.commit(o, reads, writes)
        return o

    def dma(self, queue, fn, reads=(), writes=()):
        o = self.op(queue, fn, reads, writes)
        o.is_dma = True
        if queue == "pool":
            k = self.ndma["pool"] % NPOOLSEM
        else:
            k = NPOOLSEM + self.ndma["sp"] % (NDMASEM - NPOOLSEM)
        self.ndma[queue] += 1
        self.dma_count[k] += 1
        o.sem = k
        o.val = 16 * self.dma_count[k]
        prev = self.dma_last.get(k)
        if prev is not None:
            o.deps.append(prev)
        self.dma_last[k] = o
        self.commit(o, reads, writes)
        return o

    def transfer(self, src, dst):
        pend_r = {}
        pend_d = []
        ws = []
        for b in src:
            if b.w is not None:
                ws.append(b.w)
            for e, r in b.r.items():
                pend_r.setdefault(e, []).append(r)
            pend_d.extend(b.rd)
        for b in dst:
            b.w = None
            b.r = {}
            b.rd = list(pend_d)
            for w in ws:
                if w.is_dma:
                    b.rd.append(w)
            for e, lst in pend_r.items():
                b.r[e] = max(lst, key=lambda o_: o_.seq)
            for w in ws:
                if not w.is_dma:
                    cur = b.r.get(w.eng)
                    if cur is None or w.seq > cur.seq:
                        b.r[w.eng] = w

    def finalize(self):
        for e in self.ENGS:
            c = 0
            for o in self.ops[e]:
                if (not o.is_dma) and o.sig:
                    c += 1
                    o.cnt = c

    def emit(self, eng_name, eng, eng_sems, dma_sems):
        waited = {}
        for o in self.ops[eng_name]:
            need = {}
            for d in o.deps:
                if d.is_dma:
                    key = ("d", d.sem)
                    val = d.val
                else:
                    key = ("e", d.eng)
                    val = d.cnt
                if need.get(key, 0) < val:
                    need[key] = val
            for key, val in need.items():
                if waited.get(key, 0) < val:
                    sem = dma_sems[key[1]] if key[0] == "d" else eng_sems[key[1]]
                    eng.wait_ge(sem, val)
                    waited[key] = val
            if o.fn is None:
                continue
            inst = o.fn(eng)
            if o.is_dma:
                inst.then_inc(dma_sems[o.sem], 16)
            elif o.sig:
                inst.then_inc(eng_sems[eng_name], 1)


def build_nc(nseq=NSEQ, stop_after=None):
    nc = bass.Bass("TRN2", target_bir_lowering=False)
    P = Prog()

    def din(name, shape, dt=F32):
        return nc.dram_tensor(name, list(shape), dt, kind="ExternalInput").ap()

    def dscr(name, shape, dt=BF16):
        return nc.dram_tensor(name, list(shape), dt, kind="Internal").ap()

    xin = din("x", [nseq, S, D])
    yout = nc.dram_tensor("y", [nseq, S, D], F32, kind="ExternalOutput").ap()
    cst_d = din("cst", [128, NCST])
    ident_d = din("ident", [128, 128])
    rope_d = din("rope", [128, S])
    dftc_d = din("dftc", [16, 128, 2048], BF16)
    dfts_d = din("dfts", [16, 128, 2048], BF16)
    ch_d = din("chm", [128, 1536], BF16)
    jrev_d = din("jrev", [128, 128])
    alt_d = din("alt", [1, 128], BF16)
    wfo_f = din("wfo", [D, D])
    wup_f = din("wup", [2, D, 2 * FF])
    wdn_f = din("wdn", [2, FF, D])
    win_f = din("win", [D, 512])
    wuqm_f = din("wuqm", [256, 2048])
    wukv_f = din("wukv", [128, 2048])
    wo_f = din("wo", [D, D])

    wfo_s = dscr("wfo_s", [D, D])
    wup_s = dscr("wup_s", [2, 44, 128, 1024])
    wdn_s = dscr("wdn_s", [2, FF, D])
    win_s = dscr("win_s", [D, 512])
    wuqm_s = dscr("wuqm_s", [256, 2048])
    wukv_s = dscr("wukv_s", [128, 2048])
    wo_s = dscr("wo_s", [D, D])

    b_wfo_s = Buf()
    b_wup_s = [[Buf() for _ in range(44)] for _ in range(2)]
    b_wdn_s = [[Buf() for _ in range(2)] for _ in range(2)]
    b_win_s, b_wuqm_s, b_wukv_s, b_wo_s = Buf(), Buf(), Buf(), Buf()

    off = [0]

    def alloc(nbytes):
        a = off[0]
        off[0] += (nbytes + 31) // 32 * 32
        return a

    A_X = alloc(16 * 1024 * 4)
    A_HT = alloc(8 * 2048 * 2)
    A_CST = alloc(NCST * 4)
    A_ID = alloc(128 * 4)
    A_JREV = alloc(128 * 4)
    A_ALT = alloc(128 * 2)
    A_H1024 = alloc(8 * 2)
    A_SS = alloc(16 * 4)
    A_TMP = alloc(16 * 4)
    A_RSTD = alloc(16 * 4)
    A_SS2 = alloc(16 * 2 * 4)
    A_TMP2 = alloc(16 * 2 * 4)
    A_R2 = alloc(16 * 2 * 4)
    A_MH = alloc(2 * 4)
    A_STG = [alloc(1024 * 4) for _ in range(4)]
    A_PH = off[0]
    TOTAL = 207 * 1024 + 512
    PH_BYTES = TOTAL - A_PH

    ctx = {}

    def rec_all():
        arena = ctx["arena"]
        ps = ctx["ps"]

        def view(off_b, n, dt):
            if dt == BF16:
                return arena[:, off_b // 2: off_b // 2 + n]
            return arena[:, off_b // 2: off_b // 2 + 2 * n].bitcast(F32)

        xv = view(A_X, 16 * 1024, F32).rearrange("p (t d) -> p t d", d=1024)
        hv = view(A_HT, 8 * 2048, BF16).rearrange("p (k s) -> p k s", s=2048)
        cst = view(A_CST, NCST, F32)
        ident = view(A_ID, 128, F32)
        jrev = view(A_JREV, 128, F32)
        altv = view(A_ALT, 128, BF16)
        h1024 = view(A_H1024, 8, BF16)
        ss = view(A_SS, 16, F32)
        tmp = view(A_TMP, 16, F32)
        rstd = view(A_RSTD, 16, F32)
        ss2 = view(A_SS2, 32, F32)
        tmp2 = view(A_TMP2, 32, F32)
        r2 = view(A_R2, 32, F32)
        mh = view(A_MH, 2, F32)
        stg = [view(a, 1024, F32) for a in A_STG]

        xb = [[Buf(), Buf()] for _ in range(16)]
        hb = [Buf() for _ in range(16)]
        pb = [Buf() for _ in range(8)]
        cstb, identb, mhb = Buf(), Buf(), Buf()
        jrevb, altb, h1024b, g1024b = Buf(), Buf(), Buf(), Buf()
        ssb = [Buf() for _ in range(16)]
        tmpb = [Buf() for _ in range(16)]
        rstdb = [Buf() for _ in range(16)]
        ss2b = [Buf() for _ in range(16)]
        tmp2b = [Buf() for _ in range(16)]
        r2b = [Buf() for _ in range(16)]
        stgb = [Buf() for _ in range(4)]

        def bank(b, n=1):
            return ps[:, b * 512:(b + n) * 512]

        P.dma("sp", lambda e: e.dma_start(out=cst, in_=cst_d[:, :]), writes=[cstb])
        P.dma("sp", lambda e: e.dma_start(out=ident, in_=ident_d[:, :]), writes=[identb])
        P.dma("sp", lambda e: e.dma_start(out=jrev, in_=jrev_d[:, :]), writes=[jrevb])
        P.dma("sp", lambda e: e.dma_start(out=altv[0:1, :], in_=alt_d[:, :]), writes=[altb])
        P.comp("dve", lambda e: e.memset(mh, -0.5), writes=[mhb])

        cast_q = []

        def build_cast_queue():
            cast_q.append(lambda: P.dma("pool", lambda e: e.dma_start(out=wfo_s[:, :], in_=wfo_f[:, :]),
                                        writes=[b_wfo_s]))
            for l in range(2):
                if l == 1:
                    cast_q.append(lambda: P.dma("pool", lambda e: e.dma_start(out=win_s[:, :], in_=win_f[:, :]),
                                                writes=[b_win_s]))
                    cast_q.append(lambda: P.dma("pool", lambda e: e.dma_start(out=wuqm_s[:, :], in_=wuqm_f[:, :]),
                                                writes=[b_wuqm_s]))
                    cast_q.append(lambda: P.dma("pool", lambda e: e.dma_start(out=wukv_s[:, :], in_=wukv_f[:, :]),
                                                writes=[b_wukv_s]))
                    cast_q.append(lambda: P.dma("pool", lambda e: e.dma_start(out=wo_s[:, :], in_=wo_f[:, :]),
                                                writes=[b_wo_s]))
                for jj in range(22):
                    for j in (jj, 22 + jj):
                        cast_q.append(lambda l=l, j=j: P.dma("pool", lambda e: e.dma_start(
                            out=wup_s[l, j].rearrange("p (k c) -> p k c", c=128),
                            in_=wup_f[l][:, j * 128:(j + 1) * 128].rearrange("(k p) c -> p k c", p=128)),
                            writes=[b_wup_s[l][j]]))
                    if jj == 1 or jj == 8:
                        hh = 0 if jj == 1 else 1
                        cast_q.append(lambda l=l, hh=hh: P.dma("pool", lambda e: e.dma_start(
                            out=wdn_s[l][hh * 1408:(hh + 1) * 1408, :], in_=wdn_f[l][hh * 1408:(hh + 1) * 1408, :]),
                            writes=[b_wdn_s[l][hh]]))

        def issue_casts(n):
            for _ in range(n):
                if cast_q:
                    cast_q.pop(0)()

        def load_x_tile(q, t):
            P.dma("sp", lambda e: e.dma_start(out=xv[:, t, :], in_=xin[q, t * 128:(t + 1) * 128, :]), writes=xb[t])

        def load_x(q):
            for t in range(16):
                load_x_tile(q, t)

        def rstd_ops(t, src_ss, src_b):
            P.comp("pool", lambda e, t=t: e.tensor_scalar(out=tmp[:, t:t + 1], in0=src_ss[:, t:t + 1], scalar1=1.0 / D,
                                                          scalar2=EPS, op0=ALU.mult, op1=ALU.add),
                   reads=[src_b[t]], writes=[tmpb[t]])
            P.comp("pool", lambda e, t=t: e.tensor_tensor(out=rstd[:, t:t + 1], in0=tmp[:, t:t + 1], in1=mh[:, 0:1],
                                                          op=ALU.pow),
                   reads=[tmpb[t], mhb], writes=[rstdb[t]])

        class NormStream:
            def __init__(self, n, fold=False, tr_banks=((0, 1), (2, 3)), junk=None, final_q=None, chain=None,
                         xn_alt=False):
                self.n, self.fold, self.tr_banks, self.junk, self.final_q = n, fold, tr_banks, junk, final_q
                self.chain = chain
                self.xn_alt = xn_alt
                self.LA = 2
                self.ready = -1
                self.done_rest = -1
                if fold:
                    P.comp("dve", lambda e: e.memset(hv[:, :, 1024:1025], 0.0), writes=[hb[8]])

            def sq(self, t):
                if self.junk is None:
                    gb = 4 + 2 * (t % 2)
                    out_ap, wb = bank(gb, 2), [pb[gb], pb[gb + 1]]
                else:
                    ja, jb = self.junk[t % len(self.junk)]
                    out_ap, wb = ja, [jb]
                P.comp("act", lambda e: e.activation(out=out_ap, in_=xv[:, t, :], func=AF.Square,
                                                     accum_out=ss[:, t:t + 1]),
                       reads=xb[t], writes=[ssb[t]] + wb)

            def rest(self, t):
                sl = t % 4
                if self.final_q is not None:
                    q = self.final_q
                    P.comp("dve", lambda e: e.scalar_tensor_tensor(
                        out=stg[sl], in0=xv[:, t, :], scalar=rstd[:, t:t + 1], in1=cst[:, CST_GF:CST_GF + 1024],
                        op0=ALU.mult, op1=ALU.mult), reads=xb[t] + [rstdb[t], cstb], writes=[stgb[sl]])
                    stores.append(P.dma("sp", lambda e: e.dma_start(out=yout[q, t * 128:(t + 1) * 128, :],
                                                                    in_=stg[sl]), reads=[stgb[sl]]))
                    if q + 1 < nseq:
                        load_x_tile(q + 1, t)
                    return
                n, fold = self.n, self.fold
                if self.xn_alt and t % 2 == 0:
                    P.comp("dve", lambda e: e.tensor_scalar(out=stg[sl], in0=xv[:, t, :], scalar1=rstd[:, t:t + 1],
                                                            scalar2=None, op0=ALU.mult),
                           reads=xb[t] + [rstdb[t]], writes=[stgb[sl]])
                else:
                    P.comp("pool", lambda e: e.tensor_scalar(out=stg[sl], in0=xv[:, t, :],
                                                             scalar1=rstd[:, t:t + 1], scalar2=1.0,
                                                             op0=ALU.mult, op1=ALU.mult),
                           reads=xb[t] + [rstdb[t]], writes=[stgb[sl]])
                b0, b1 = self.tr_banks[t % len(self.tr_banks)]
                assert b1 == b0 + 1
                rev = fold and t >= 8

                def tr(e):
                    for kc in range(8):
                        o_ = ps[:, b0 * 512 + kc * 128: b0 * 512 + (kc + 1) * 128]
                        if rev:
                            i = e.matmul(o_, lhsT=stg[sl][:, kc * 128:(kc + 1) * 128], rhs=jrev, start=True, stop=True)
                        else:
                            i = e.transpose(o_, stg[sl][:, kc * 128:(kc + 1) * 128], ident)
                    return i
                P.comp("pe", tr, reads=[stgb[sl], identb, jrevb], writes=[pb[b0], pb[b0 + 1]])
                psT = bank(b0, 2).rearrange("p (a b) -> p a b", b=128)
                gsl = cst[:, n * 8:(n + 1) * 8]
                if not rev:
                    P.comp("dve", lambda e: e.tensor_tensor(
                        out=hv[:, :, t * 128:(t + 1) * 128], in0=psT,
                        in1=gsl.unsqueeze(2).to_broadcast([128, 8, 128]), op=ALU.mult),
                        reads=[pb[b0], pb[b0 + 1], cstb], writes=[hb[t]])
                elif t >= 9:
                    a_ = 3072 - 128 * t - 127
                    P.comp("dve", lambda e: e.tensor_tensor(
                        out=hv[:, :, a_:a_ + 128], in0=psT,
                        in1=gsl.unsqueeze(2).to_broadcast([128, 8, 128]), op=ALU.mult),
                        reads=[pb[b0], pb[b0 + 1], cstb], writes=[hb[a_ // 128], hb[(a_ + 127) // 128]])
                else:
                    P.comp("dve", lambda e: e.tensor_tensor(
                        out=hv[:, :, 1921:2048], in0=psT[:, :, 0:127],
                        in1=gsl.unsqueeze(2).to_broadcast([128, 8, 127]), op=ALU.mult),
                        reads=[pb[b0], pb[b0 + 1], cstb], writes=[hb[15]])
                    P.comp("dve", lambda e: e.tensor_tensor(
                        out=h1024.unsqueeze(2), in0=psT[:, :, 127:128],
                        in1=gsl.unsqueeze(2), op=ALU.mult),
                        reads=[pb[b0], pb[b0 + 1], cstb], writes=[h1024b])

            def advance(self, t):
                while self.ready < t:
                    self.ready += 1
                    r = self.ready - self.LA
                    if r >= 0:
                        self.rest(r)
                        self.done_rest = r
                    if self.ready - 1 >= 0:
                        rstd_ops(self.ready - 1, ss, ssb)
                    self.sq(self.ready)

            def finish(self):
                self.advance(15)
                rstd_ops(15, ss, ssb)
                while self.done_rest < 15:
                    self.done_rest += 1
                    self.rest(self.done_rest)
                if self.chain is not None:
                    self.chain.finish()

        def phase_norm(n, fold=False):
            ns = NormStream(n, fold, xn_alt=True)
            ns.finish()

        def phase_fourier():
            o = A_PH
            Ec = view(o, 8 * 1024, BF16).rearrange("p (t d) -> p t d", d=1024); o += 16384
            Es = view(o, 8 * 1024, BF16).rearrange("p (t d) -> p t d", d=1024); o += 16384
            chv = view(o, 1536, BF16).rearrange("p (m k c) -> p m k c", m=3, k=2); o += 3072
            dcv, dsv = [], []
            for _ in range(2):
                dcv.append(view(o, 1024, BF16).rearrange("p (k m) -> p k m", m=128)); o += 2048
                dsv.append(view(o, 1024, BF16).rearrange("p (k m) -> p k m", m=128)); o += 2048
            ftok = [stg[0], stg[1]]
            fT = []
            for _ in range(2):
                fT.append(view(o, 1024, BF16).rearrange("p (k m) -> p k m", m=128)); o += 2048
            g1024 = view(o, 1024, BF16); o += 2048
            wfo = view(o, 8 * 1024, BF16).rearrange("p (k n) -> p k n", n=1024); o += 16384
            fjunk = []
            for _ in range(2):
                fjunk.append((view(o, 1024, BF16), Buf())); o += 2048
            assert o - A_PH <= PH_BYTES, (o - A_PH, PH_BYTES)
            Ecb = [Buf() for _ in range(8)]
            Esb = [Buf() for _ in range(8)]
            chb, wfob = Buf(), Buf()
            dcb = [Buf(), Buf()]
            dsb = [Buf(), Buf()]
            ftokb = [stgb[0], stgb[1]]
            fTb = [Buf(), Buf()]
            allph = Ecb + Esb + [chb] + dcb + dsb + fTb + [g1024b, wfob] + [jb for _, jb in fjunk]
            P.transfer(ctx["phase_bufs"], allph)
            ctx["phase_bufs"] = allph

            P.dma("sp", lambda e: e.dma_start(out=chv.rearrange("p m k c -> p (m k c)"), in_=ch_d[:, :]), writes=[chb])

            def load_dft(j):
                sl = j % 2
                P.dma("sp", lambda e: e.dma_start(out=dcv[sl].rearrange("p k m -> p (k m)"), in_=dftc_d[j][:, 0:1024]),
                      writes=[dcb[sl]])
                P.dma("sp", lambda e: e.dma_start(out=dsv[sl].rearrange("p k m -> p (k m)"), in_=dfts_d[j][:, 0:1024]),
                      writes=[dsb[sl]])
            load_dft(0)
            load_dft(1)
            for k in range(8):
                b = 4 * (k % 2)

                def mm(e, k=k, b=b):
                    for g in range(4):
                        oc = ps[:, b * 512 + g * 256: b * 512 + (g + 1) * 256]
                        os_ = ps[:, (b + 2) * 512 + g * 256: (b + 2) * 512 + (g + 1) * 256]
                        n_ = 0
                        for mir in range(2):
                            c0_ = 128 * k if mir == 0 else 1024 + 128 * k
                            for kl in range(2):
                                kc = 2 * g + kl
                                lhsT = hv[:, kc, c0_:c0_ + 128]
                                e.matmul(oc, lhsT=lhsT, rhs=chv[:, 0, kl, :], start=(n_ == 0), stop=(n_ == 3))
                                i = e.matmul(os_, lhsT=lhsT, rhs=chv[:, 1 + mir, kl, :], start=(n_ == 0), stop=(n_ == 3))
                                n_ += 1
                    return i
                P.comp("pe", mm, reads=[hb[k], hb[8 + k], chb], writes=pb[b:b + 4])
                P.comp("act", lambda e, k=k, b=b: e.activation(out=Ec[:, k, :], in_=bank(b, 2), func=AF.Copy),
                       reads=pb[b:b + 2], writes=[Ecb[k]])
                P.comp("dve", lambda e, k=k, b=b: e.tensor_copy(out=Es[:, k, :], in_=bank(b + 2, 2)),
                       reads=pb[b + 2:b + 4], writes=[Esb[k]])

            def mm1024(e):
                for g in range(4):
                    for kl in range(2):
                        kc = 2 * g + kl
                        i = e.matmul(ps[0:1, g * 256:(g + 1) * 256], lhsT=h1024[:, kc:kc + 1], rhs=chv[:, 0, kl, :],
                                     start=(kl == 0), stop=(kl == 1))
                return i
            P.comp("pe", mm1024, reads=[h1024b, chb], writes=pb[0:2])
            P.comp("dve", lambda e: e.tensor_copy(out=g1024[0:1, :], in_=ps[0:1, 0:1024]), reads=pb[0:2], writes=[g1024b])
            if stop_after == "B1":
                return
            P.dma("sp", lambda e: e.dma_start(out=wfo, in_=wfo_s.rearrange("(k p) n -> p k n", p=128)),
                  reads=[b_wfo_s], writes=[wfob])

            def f_half(j, half):
                sl = j % 2
                fb = 2 * (j % 2)

                def mm(e):
                    o_ = ps[:, (fb + half) * 512:(fb + half + 1) * 512]
                    for k in range(8):
                        e.matmul(o_, lhsT=dcv[sl][:, k, :], rhs=Ec[:, k, half * 512:(half + 1) * 512],
                                 start=(k == 0), stop=False)
                    for k in range(8):
                        e.matmul(o_, lhsT=dsv[sl][:, k, :], rhs=Es[:, k, half * 512:(half + 1) * 512],
                                 start=False, stop=False)
                    return e.matmul(o_, lhsT=altv[0:1, :], rhs=g1024[0:1, half * 512:(half + 1) * 512],
                                    start=False, stop=True)
                P.comp("pe", mm, reads=[dcb[sl], dsb[sl], altb, g1024b] + Ecb + Esb, writes=[pb[fb + half]])
                if half == 1:
                    if j + 2 < 16:
                        load_dft(j + 2)
                    P.comp("act", lambda e: e.activation(out=ftok[sl], in_=bank(fb, 2), func=AF.Copy),
                           reads=pb[fb:fb + 2], writes=[ftokb[sl]])

            def t_op(j):
                sl = j % 2

                def tr(e):
                    for kc in range(8):
                        i = e.transpose(ps[:, 4 * 512 + kc * 128: 4 * 512 + (kc + 1) * 128],
                                        ftok[sl][:, kc * 128:(kc + 1) * 128], ident)
                    return i
                P.comp("pe", tr, reads=[ftokb[sl], identb], writes=pb[4:6])
                P.comp("dve", lambda e: e.tensor_copy(out=fT[sl], in_=bank(4, 2).rearrange("p (a b) -> p a b", b=128)),
                       reads=pb[4:6], writes=[fTb[sl]])

            def y_op(j):
                sl = j % 2

                def mm(e):
                    for half in range(2):
                        for kc in range(8):
                            i = e.matmul(bank(6 + half), lhsT=fT[sl][:, kc, :],
                                         rhs=wfo[:, kc, half * 512:(half + 1) * 512], start=(kc == 0), stop=(kc == 7))
                    return i
                P.comp("pe", mm, reads=[fTb[sl], wfob], writes=pb[6:8])
                P.comp("dve", lambda e: e.tensor_tensor(out=xv[:, j, :], in0=bank(6, 2), in1=xv[:, j, :], op=ALU.add),
                       reads=pb[6:8] + xb[j], writes=xb[j])
            ns = NormStream(1, tr_banks=((4, 5),), junk=fjunk)
            f_half(0, 0)
            f_half(0, 1)
            for j in range(16):
                if j + 1 < 16:
                    f_half(j + 1, 0)
                t_op(j)
                if j + 1 < 16:
                    f_half(j + 1, 1)
                y_op(j)
                ns.advance(j)
                issue_casts(3)
            ns.finish()

        def phase_ffn(l, next_norm):
            o = A_PH
            gp = []
            for _ in range(2):
                gp.append(view(o, 5 * 2048, BF16).rearrange("p (j s) -> p j s", s=2048)); o += 5 * 4096
            wdn = []
            for _ in range(2):
                wdn.append(view(o, 5 * 1024, BF16).rearrange("p (j n) -> p j n", n=1024)); o += 5 * 2048
            wup = []
            for _ in range(3):
                wup.append(view(o, 2048, BF16).rearrange("p (g k c) -> p g k c", g=2, k=8)); o += 4096
            accG, accV = [], []
            for _ in range(2):
                accG.append(view(o, 1024, F32)); o += 4096
                accV.append(view(o, 1024, F32)); o += 4096
            assert o - A_PH <= PH_BYTES, (o - A_PH, PH_BYTES)
            gpb = [[[Buf(), Buf()] for _ in range(5)] for _ in range(2)]
            wdnb = [Buf(), Buf()]
            wupb = [[Buf(), Buf()] for _ in range(3)]
            accGb = [Buf(), Buf()]
            accVb = [Buf(), Buf()]
            allph = [b for s_ in gpb for jj in s_ for b in jj] + wdnb + [b for s_ in wupb for b in s_] + accGb + accVb
            P.transfer(ctx["phase_bufs"], allph)
            ctx["phase_bufs"] = allph

            def cp(j, k):
                c = CST_CONV + (l * 44 + j) * 4 + k
                return cst[:, c:c + 1]

            chunks = [(p, jj, j) for p, part in enumerate(PARTS) for jj, j in enumerate(part)]

            def load_wup(ci):
                if ci >= len(chunks):
                    return
                _, _, j = chunks[ci]
                slot = ci % 3
                P.dma("sp", lambda e: e.dma_start(out=wup[slot][:, 0].rearrange("p k c -> p (k c)"), in_=wup_s[l, j]),
                      reads=[b_wup_s[l][j]], writes=[wupb[slot][0]])
                P.dma("sp", lambda e: e.dma_start(out=wup[slot][:, 1].rearrange("p k c -> p (k c)"),
                                                  in_=wup_s[l, 22 + j]),
                      reads=[b_wup_s[l][22 + j]], writes=[wupb[slot][1]])

            def load_wdn(p):
                part = PARTS[p]
                j0, n = part[0], len(part)
                slot = p % 2
                P.dma("sp", lambda e: e.dma_start(
                    out=wdn[slot][:, 0:n, :],
                    in_=wdn_s[l][j0 * 128:(j0 + n) * 128, :].rearrange("(f p) n -> p f n", p=128)),
                    reads=b_wdn_s[l], writes=[wdnb[slot]])

            def conv_evac(src_base, acc, accb, j_cst, hf):
                pbs = pb[src_base:src_base + 4]
                U = ps[:, src_base * 512:(src_base + 4) * 512]
                P.comp("act", lambda e: e.activation(out=acc, in_=U[:, hf * 1024:(hf + 1) * 1024], func=AF.Identity,
                                                     scale=cp(j_cst, 1), bias=cp(j_cst, 3)),
                       reads=pbs[2 * hf:2 * hf + 2] + [cstb], writes=[accb])

            def conv_taps(src_base, acc, accb, j_cst, hf):
                pbs = pb[src_base:src_base + 4]
                U = ps[:, src_base * 512:(src_base + 4) * 512]
                if hf == 0:
                    o0, i0 = acc[:, 1:1024], U[:, 0:1023]
                    o2, i2 = acc[:, 0:1024], U[:, 1:1025]
                else:
                    o0, i0 = acc[:, 0:1024], U[:, 1023:2047]
                    o2, i2 = acc[:, 0:1023], U[:, 1025:2048]
                P.comp("dve", lambda e: e.scalar_tensor_tensor(out=o0, in0=i0, scalar=cp(j_cst, 0), in1=o0,
                                                               op0=ALU.mult, op1=ALU.add),
                       reads=pbs + [accb, cstb], writes=[accb])
                P.comp("dve", lambda e: e.scalar_tensor_tensor(out=o2, in0=i2, scalar=cp(j_cst, 2), in1=o2,
                                                               op0=ALU.mult, op1=ALU.add),
                       reads=pbs + [accb, cstb], writes=[accb])

            def up_chunk(ci):
                p, jj, j = chunks[ci]
                slot = ci % 3
                gs = p % 2
                issue_casts(3)
                load_wup(ci + 2)
                for gv in range(2):
                    base = 4 * gv

                    def mm(e, gv=gv, base=base):
                        for kc in range(8):
                            for qt in range(4):
                                i = e.matmul(bank(base + qt), lhsT=wup[slot][:, gv, kc, :],
                                             rhs=hv[:, kc, qt * 512:(qt + 1) * 512], start=(kc == 0), stop=(kc == 7))
                        return i
                    P.comp("pe", mm, reads=[wupb[slot][gv]] + hb, writes=pb[base:base + 4])
                for hf in range(2):
                    conv_evac(0, accG[hf], accGb[hf], j, hf)
                for hf in range(2):
                    conv_taps(0, accG[hf], accGb[hf], j, hf)
                for hf in range(2):
                    conv_evac(4, accV[hf], accVb[hf], 22 + j, hf)
                for hf in range(2):
                    conv_taps(4, accV[hf], accVb[hf], 22 + j, hf)
                for hf in range(2):
                    P.comp("act", lambda e, hf=hf: e.activation(out=accG[hf], in_=accG[hf], func=AF.Silu),
                           reads=[accGb[hf]], writes=[accGb[hf]])
                for hf in range(2):
                    P.comp("pool", lambda e, hf=hf: e.tensor_tensor(out=gp[gs][:, jj, hf * 1024:(hf + 1) * 1024],
                                                                    in0=accG[hf], in1=accV[hf], op=ALU.mult),
                           reads=[accGb[hf], accVb[hf]], writes=[gpb[gs][jj][hf]])

            cnt = [0]

            def down_part(p):
                n = len(PARTS[p])
                gs = p % 2
                slot = p % 2
                last = (p == len(PARTS) - 1)
                ns = None
                if last:
                    junk = [(accG[0], accGb[0]), (accG[1], accGb[1])]
                    if next_norm == "final":
                        nxt = None
                        if ctx["q"] + 1 < nseq:
                            nxt = NormStream(0, fold=True, tr_banks=((4, 5), (6, 7)),
                                             junk=[(accV[0], accVb[0]), (accV[1], accVb[1])], xn_alt=True)
                        ns = NormStream(None, final_q=ctx["q"], junk=junk, chain=nxt)
                    else:
                        ns = NormStream(next_norm, tr_banks=((4, 5), (6, 7)), junk=junk)
                for t in range(16):
                    for half in range(2):
                        bk = cnt[0] % (4 if last else 8)
                        cnt[0] += 1

                        def mm(e, t=t, half=half, bk=bk):
                            for jj in range(n):
                                i = e.matmul(bank(bk), lhsT=gp[gs][:, jj, t * 128:(t + 1) * 128],
                                             rhs=wdn[slot][:, jj, half * 512:(half + 1) * 512],
                                             start=(jj == 0), stop=(jj == n - 1))
                            return i
                        P.comp("pe", mm, reads=[gpb[gs][jj][t // 8] for jj in range(n)] + [wdnb[slot]],
                               writes=[pb[bk]])
                        P.comp("dve", lambda e, t=t, half=half, bk=bk: e.tensor_tensor(
                            out=xv[:, t, half * 512:(half + 1) * 512], in0=bank(bk),
                            in1=xv[:, t, half * 512:(half + 1) * 512], op=ALU.add),
                            reads=[pb[bk], xb[t][half]], writes=[xb[t][half]])
                    if ns is not None:
                        ns.advance(t)
                if ns is not None:
                    ns.finish()

            load_wup(0)
            load_wup(1)
            load_wdn(0)
            ci = 0
            for p in range(len(PARTS)):
                if p + 1 < len(PARTS):
                    load_wdn(p + 1)
                start = 1 if p > 0 else 0
                for jj in range(start, len(PARTS[p])):
                    up_chunk(ci)
                    ci += 1
                if p + 1 < len(PARTS):
                    up_chunk(ci)
                    ci += 1
                down_part(p)

        def phase_mla():
            o = A_PH
            cqT = view(o, 2 * 2048, BF16).rearrange("p (k s) -> p k s", s=2048); o += 8192
            ckvT = view(o, 2048, BF16); o += 4096
            krT = view(o, 2048, BF16); o += 4096
            rope = view(o, 2048, F32); o += 8192
            t12 = []
            for _ in range(2):
                t12.append(view(o, 512, F32)); o += 2048
            wuqm = view(o, 2 * 2048, BF16).rearrange("p (k n) -> p k n", n=2048); o += 8192
            wukv = view(o, 2048, BF16); o += 4096
            oT = []
            for _ in range(2):
                oT.append(view(o, 2 * 2048, BF16).rearrange("p (c s) -> p c s", s=2048)); o += 8192
            o_d = o
            win = view(o, 8 * 512, BF16).rearrange("p (k n) -> p k n", n=512); o += 8192
            an = []
            for _ in range(4):
                an.append(view(o, 384, F32)); o += 1536
            o1 = o
            o = o_d
            pT = []
            for _ in range(3):
                pT.append(view(o, 1024, BF16)); o += 2048
            rec = []
            for _ in range(2):
                rec.append(view(o, 512, F32)); o += 2048
            wo = []
            for _ in range(2):
                wo.append(view(o, 2 * 1024, BF16).rearrange("p (c n) -> p c n", n=1024)); o += 4096
            vaug1 = view(o, 16 * 384, BF16).rearrange("p (t c) -> p t c", c=384); o += 12288
            qscr2 = view(o, 512, F32); o += 2048
            assert max(o, o1) - A_PH <= PH_BYTES, (max(o, o1) - A_PH, PH_BYTES)
            oh = A_HT
            Qh, Kh = [], []
            for _ in range(2):
                Qh.append(view(oh, 2048, BF16)); oh += 4096
                Kh.append(view(oh, 2048, BF16)); oh += 4096
            vaug = view(oh, 16 * 384, BF16).rearrange("p (t c) -> p t c", c=384); oh += 12288
            assert oh - A_HT <= 32768
            vaug2 = [vaug, vaug1]

            cqTb = [Buf() for _ in range(16)]
            ckvTb = [Buf() for _ in range(16)]
            krTb = [Buf() for _ in range(4)]
            ropeb, wuqmb, wukvb, winb = Buf(), Buf(), Buf(), Buf()
            t12b = [Buf(), Buf()]
            oTb = [[[Buf() for _ in range(4)] for _ in range(2)] for _ in range(2)]
            oTb2 = [[[[Buf() for _ in range(2)] for _ in range(4)] for _ in range(2)] for _ in range(2)]
            anb = [Buf() for _ in range(4)]
            pTb = [Buf(), Buf(), Buf()]
            recb = [Buf(), Buf()]
            recq = [[Buf() for _ in range(4)] for _ in range(2)]
            t12h = [[Buf(), Buf()], [Buf(), Buf()]]
            wob = [Buf(), Buf()]
            Qhb = [[Buf() for _ in range(4)] for _ in range(2)]
            Khb = [[Buf() for _ in range(4)] for _ in range(2)]
            Khrb = [Buf(), Buf()]
            vaugb = [[Buf() for _ in range(16)] for _ in range(2)]
            onesb = [Buf(), Buf()]
            d1 = [winb] + anb
            rest = (cqTb + ckvTb + krTb + [ropeb, wuqmb, wukvb] + t12b +
                    [b for a in oTb2 for c in a for d_ in c for b in d_])
            P.transfer(ctx["phase_bufs"], d1 + rest)
            d2 = pTb + recb + wob + vaugb[1] + [onesb[1]] + [b for r_ in recq for b in r_]
            ctx["phase_bufs"] = rest + d2

            P.dma("sp", lambda e: e.dma_start(out=win, in_=win_s.rearrange("(k p) n -> p k n", p=128)),
                  reads=[b_win_s], writes=[winb])
            P.dma("sp", lambda e: e.dma_start(out=rope, in_=rope_d[:, :]), writes=[ropeb])
            P.dma("sp", lambda e: e.dma_start(out=wuqm, in_=wuqm_s.rearrange("(k p) n -> p k n", p=128)),
                  reads=[b_wuqm_s], writes=[wuqmb])
            P.dma("sp", lambda e: e.dma_start(out=wukv, in_=wukv_s[:, :]), reads=[b_wukv_s], writes=[wukvb])

            def d1_a(t):
                bk = t % 4
                sl = t % 4

                def mm(e):
                    for kc in range(8):
                        i = e.matmul(ps[:, bk * 512: bk * 512 + 384], lhsT=hv[:, kc, t * 128:(t + 1) * 128],
                                     rhs=win[:, kc, 0:384], start=(kc == 0), stop=(kc == 7))
                    return i
                P.comp("pe", mm, reads=[hb[t], winb], writes=[pb[bk]])
                P.comp("act", lambda e: e.activation(
                    out=an[sl][:, 0:256], in_=ps[:, bk * 512: bk * 512 + 256],
                    func=AF.Square, accum_out=ss2[:, 2 * t:2 * t + 1]),
                    reads=[pb[bk]], writes=[ss2b[t], anb[sl]])
                P.comp("act", lambda e: e.activation(
                    out=an[sl][:, 256:384], in_=ps[:, bk * 512 + 256: bk * 512 + 384],
                    func=AF.Square, accum_out=ss2[:, 2 * t + 1:2 * t + 2]),
                    reads=[pb[bk], ss2b[t], anb[sl]], writes=[ss2b[t], anb[sl]])
                P.comp("pool", lambda e: e.tensor_scalar(out=tmp2[:, 2 * t:2 * t + 1], in0=ss2[:, 2 * t:2 * t + 1],
                                                         scalar1=1.0 / 256, scalar2=EPS, op0=ALU.mult, op1=ALU.add),
                       reads=[ss2b[t]], writes=[tmp2b[t]])
                P.comp("pool", lambda e: e.tensor_scalar(out=tmp2[:, 2 * t + 1:2 * t + 2],
                                                         in0=ss2[:, 2 * t + 1:2 * t + 2],
                                                         scalar1=1.0 / 128, scalar2=EPS, op0=ALU.mult, op1=ALU.add),
                       reads=[ss2b[t], tmp2b[t]], writes=[tmp2b[t]])
                P.comp("pool", lambda e: e.tensor_tensor(out=r2[:, 2 * t:2 * t + 2], in0=tmp2[:, 2 * t:2 * t + 2],
                                                         in1=mh[:, 0:2], op=ALU.pow),
                       reads=[tmp2b[t], mhb], writes=[r2b[t]])

            def d1_b(t):
                bk = t % 4
                sl = t % 4
                b2 = 4 + (t % 2)
                P.comp("dve", lambda e: e.tensor_scalar(
                    out=an[sl][:, 0:256], in0=ps[:, bk * 512: bk * 512 + 256], scalar1=r2[:, 2 * t:2 * t + 1],
                    scalar2=None, op0=ALU.mult), reads=[pb[bk], r2b[t]], writes=[anb[sl]])
                P.comp("dve", lambda e: e.tensor_scalar(
                    out=an[sl][:, 256:384], in0=ps[:, bk * 512 + 256: bk * 512 + 384],
                    scalar1=r2[:, 2 * t + 1:2 * t + 2], scalar2=None, op0=ALU.mult),
                    reads=[pb[bk], r2b[t], anb[sl]], writes=[anb[sl]])

                def tr(e):
                    for c in range(3):
                        i = e.transpose(ps[:, b2 * 512 + c * 128: b2 * 512 + (c + 1) * 128],
                                        an[sl][:, c * 128:(c + 1) * 128], ident)
                    return i
                P.comp("pe", tr, reads=[anb[sl], identb], writes=[pb[b2]])

            def d1_c(t):
                b2 = 4 + (t % 2)
                P.comp("dve", lambda e: e.tensor_tensor(
                    out=cqT[:, :, t * 128:(t + 1) * 128],
                    in0=ps[:, b2 * 512: b2 * 512 + 256].rearrange("p (a b) -> p a b", b=128),
                    in1=cst[:, 32:34].unsqueeze(2).to_broadcast([128, 2, 128]), op=ALU.mult),
                    reads=[pb[b2], cstb], writes=[cqTb[t]])
                P.comp("dve", lambda e: e.tensor_scalar(
                    out=ckvT[:, t * 128:(t + 1) * 128], in0=ps[:, b2 * 512 + 256: b2 * 512 + 384],
                    scalar1=cst[:, 34:35], scalar2=None, op0=ALU.mult),
                    reads=[pb[b2], cstb], writes=[ckvTb[t]])

            def kr_qt(qt):
                bA = 6
                bB = 7

                def mm(e):
                    for kc in range(8):
                        e.matmul(ps[0:96, bA * 512:(bA + 1) * 512], lhsT=win[:, kc, 384:480],
                                 rhs=hv[:, kc, qt * 512:(qt + 1) * 512], start=(kc == 0), stop=(kc == 7))
                    for kc in range(8):
                        i = e.matmul(ps[0:32, bB * 512:(bB + 1) * 512], lhsT=win[:, kc, 480:512],
                                     rhs=hv[:, kc, qt * 512:(qt + 1) * 512], start=(kc == 0), stop=(kc == 7))
                    return i
                P.comp("pe", mm, reads=[winb] + hb[4 * qt:4 * qt + 4], writes=[pb[bA], pb[bB]])
                P.comp("dve", lambda e: e.tensor_tensor(
                    out=t12[0][64:96, :], in0=ps[64:96, bA * 512:(bA + 1) * 512],
                    in1=rope[64:96, qt * 512:(qt + 1) * 512], op=ALU.mult),
                    reads=[pb[bA], ropeb], writes=[t12b[0]])
                P.comp("dve", lambda e: e.tensor_tensor(
                    out=t12[1][64:96, :], in0=ps[0:32, bB * 512:(bB + 1) * 512],
                    in1=rope[96:128, qt * 512:(qt + 1) * 512], op=ALU.mult),
                    reads=[pb[bB], ropeb], writes=[t12b[1]])
                P.comp("dve", lambda e: e.tensor_tensor(
                    out=krT[64:96, qt * 512:(qt + 1) * 512], in0=t12[0][64:96, :], in1=t12[1][64:96, :], op=ALU.add),
                    reads=t12b, writes=[krTb[qt]])

            for t in range(16 + 3):
                if t < 16:
                    d1_a(t)
                if 0 <= t - 2 < 16:
                    d1_b(t - 2)
                if 0 <= t - 3 < 16:
                    d1_c(t - 3)
                if t in (2, 6, 10, 14):
                    kr_qt(t // 4)

            hal = [b for s_ in Qhb for b in s_] + [b for s_ in Khb for b in s_] + Khrb + vaugb[0] + [onesb[0]]
            P.transfer(hb, hal)
            P.transfer(d1, d2)
            t12h_flat = [b for r_ in t12h for b in r_]
            P.transfer(t12b, t12h_flat)
            qscr = [t12[0], qscr2]
            qscrb = [t12h[0][0], Buf()]
            P.transfer(d1, [qscrb[1]])
            qctr = [0]
            ctx["phase_bufs"] = ctx["phase_bufs"] + t12h_flat + [qscrb[1]]
            for vs in range(2):
                vones = vaug2[vs].rearrange("p t (a b c) -> p t a b c", a=2, b=3)[:, :, :, 1, :]
                P.comp("pool", lambda e, vones=vones: e.memset(vones, 1.0), writes=[onesb[vs]])

            sbk = [0]
            bk_pool = [list(range(8))]

            def nextbk():
                sbk[0] += 1
                pool_ = bk_pool[0]
                return pool_[sbk[0] % len(pool_)]

            def kr_unit(h):
                hs = h % 2
                P.comp("pool", lambda e: e.tensor_copy(out=Kh[hs][64:96, :], in_=krT[64:96, :]),
                       reads=krTb, writes=[Khrb[hs]])

            def q_unit(h, qt):
                hs = h % 2
                bk = nextbk()
                c0_ = qt * 512
                qs = qctr[0] % 2
                qctr[0] += 1
                scr = qscr[qs]

                def mmq(e):
                    for kc in range(2):
                        i = e.matmul(bank(bk), lhsT=wuqm[:, kc, h * 128:(h + 1) * 128],
                                     rhs=cqT[:, kc, c0_:c0_ + 512], start=(kc == 0), stop=(kc == 1))
                    return i
                P.comp("pe", mmq, reads=[wuqmb] + cqTb[4 * qt:4 * qt + 4], writes=[pb[bk]])
                P.comp("dve", lambda e: e.tensor_tensor(
                    out=scr[0:96, :], in0=ps[0:96, bk * 512:(bk + 1) * 512],
                    in1=rope[0:96, c0_:c0_ + 512], op=ALU.mult),
                    reads=[pb[bk], ropeb], writes=[qscrb[qs]])
                P.comp("dve", lambda e: e.tensor_tensor(
                    out=t12[1][64:96, :], in0=ps[96:128, bk * 512:(bk + 1) * 512],
                    in1=rope[96:128, c0_:c0_ + 512], op=ALU.mult),
                    reads=[pb[bk], ropeb], writes=[t12h[1][0]])
                P.comp("pool", lambda e: e.tensor_tensor(
                    out=Qh[hs][64:96, c0_:c0_ + 512], in0=scr[64:96, :], in1=t12[1][64:96, :], op=ALU.add),
                    reads=[qscrb[qs], t12h[1][0], Qhb[hs][qt]], writes=[Qhb[hs][qt]])
                P.comp("pool", lambda e: e.tensor_copy(out=Qh[hs][0:64, c0_:c0_ + 512], in_=scr[0:64, :]),
                       reads=[qscrb[qs], Qhb[hs][qt]], writes=[Qhb[hs][qt]])

            def k_unit(h, qt):
                hs = h % 2
                bk = nextbk()
                P.comp("pe", lambda e: e.matmul(
                    ps[0:64, bk * 512:(bk + 1) * 512], lhsT=wukv[:, h * 64:(h + 1) * 64],
                    rhs=ckvT[:, qt * 512:(qt + 1) * 512], start=True, stop=True),
                    reads=[wukvb] + ckvTb[4 * qt:4 * qt + 4], writes=[pb[bk]])
                P.comp("dve", lambda e: e.tensor_copy(
                    out=Kh[hs][0:64, qt * 512:(qt + 1) * 512], in_=ps[0:64, bk * 512:(bk + 1) * 512]),
                    reads=[pb[bk]], writes=[Khb[hs][qt]])

            def v_unit(hg, t):
                vs = hg % 2
                bk = nextbk()
                P.comp("pe", lambda e: e.matmul(
                    ps[:, bk * 512: bk * 512 + 256], lhsT=ckvT[:, t * 128:(t + 1) * 128],
                    rhs=wukv[:, 1024 + hg * 256: 1024 + (hg + 1) * 256], start=True, stop=True),
                    reads=[ckvTb[t], wukvb], writes=[pb[bk]])
                P.comp("dve", lambda e: e.tensor_copy(
                    out=vaug2[vs][:, t, :].rearrange("p (a b c) -> p a b c", a=2, b=3)[:, :, 0:3:2, :],
                    in_=ps[:, bk * 512: bk * 512 + 256].rearrange("p (a b c) -> p a b c", a=2, b=2)),
                    reads=[pb[bk]], writes=[vaugb[vs][t]])

            def wo_load(hg):
                wsl = hg % 2
                P.dma("sp", lambda e: e.dma_start(
                    out=wo[wsl], in_=wo_s[hg * 256:(hg + 1) * 256, :].rearrange("(c p) n -> p c n", p=128)),
                    reads=[b_wo_s], writes=[wob[wsl]])

            def wo_unit(hg, t, half):
                osl = hg % 2
                wsl = hg % 2
                bk = nextbk()

                def mm(e):
                    for c in range(2):
                        i = e.matmul(bank(bk), lhsT=oT[osl][:, c, t * 128:(t + 1) * 128],
                                     rhs=wo[wsl][:, c, half * 512:(half + 1) * 512], start=(c == 0), stop=(c == 1))
                    return i
                P.comp("pe", mm, reads=[oTb2[osl][c][t // 4][od] for c in range(2) for od in range(2)] + [wob[wsl]],
                       writes=[pb[bk]])
                P.comp("dve", lambda e: e.tensor_tensor(
                    out=xv[:, t, half * 512:(half + 1) * 512], in0=bank(bk),
                    in1=xv[:, t, half * 512:(half + 1) * 512], op=ALU.add),
                    reads=[pb[bk], xb[t][half]], writes=[xb[t][half]])

            its = [(h, qt, k2) for h in range(NH) for qt in range(4) for k2 in range(8)]
            NIT = len(its)

            def s_op(g):
                h, qt, k2 = its[g]
                hs = h % 2
                sb = 2 * (g % 2)

                def mm(e):
                    for kk in range(2):
                        kt = 2 * k2 + kk
                        i = e.matmul(bank(sb + kk), lhsT=Kh[hs][0:96, kt * 128:(kt + 1) * 128],
                                     rhs=Qh[hs][0:96, qt * 512:(qt + 1) * 512], start=True, stop=True)
                    return i
                P.comp("pe", mm, reads=[Khb[hs][k2 // 2], Khrb[hs], Qhb[hs][qt]], writes=pb[sb:sb + 2])

            def e_op(g):
                sb = 2 * (g % 2)
                psl = g % 3
                P.comp("act", lambda e: e.activation(out=pT[psl], in_=bank(sb, 2), func=AF.Exp, scale=ATTN_SCALE),
                       reads=pb[sb:sb + 2], writes=[pTb[psl]])
                return psl

            def pv_op(g, psl):
                h, qt, k2 = its[g]
                vs = (h // 4) % 2
                hl = h % 4
                c0 = (hl // 2) * 192 + (hl % 2) * 64
                pob = 4 + ((g // 8) % 2)

                def mm(e):
                    for kk in range(2):
                        kt = 2 * k2 + kk
                        i = e.matmul(bank(pob), lhsT=vaug2[vs][:, kt, c0:c0 + 128],
                                     rhs=pT[psl][:, kk * 512:(kk + 1) * 512], start=(kt == 0), stop=(kt == 15))
                    return i
                P.comp("pe", mm, reads=[vaugb[vs][2 * k2], vaugb[vs][2 * k2 + 1], onesb[vs], pTb[psl]],
                       writes=[pb[pob]])

            def norm_items(g):
                h, qt, _ = its[g]
                hg, hl = h // 4, h % 4
                osl = hg % 2
                pair, odd = hl // 2, hl % 2
                orow = slice(64, 128) if odd else slice(0, 64)
                srow = slice(0, 64) if odd else slice(64, 128)
                pob = 4 + ((g // 8) % 2)
                rs = (g // 8) % 2
                items = []
                if False:
                    def ln_():
                        P.comp("act", lambda e: e.activation(out=rec[rs][srow, :],
                                                             in_=ps[srow, pob * 512:(pob + 1) * 512], func=AF.Ln),
                               reads=[pb[pob]], writes=recq[rs])

                    def ex_():
                        P.comp("act", lambda e: e.activation(out=rec[rs][srow, :], in_=rec[rs][srow, :],
                                                             func=AF.Exp, scale=-1.0),
                               reads=recq[rs], writes=recq[rs])
                    items.append(ln_)
                    items.append(ex_)
                for c in range(4 if not items else 0):
                    def rc(c=c):
                        P.comp("dve", lambda e: e.reciprocal(
                            out=rec[rs][srow, c * 128:(c + 1) * 128],
                            in_=ps[srow, pob * 512 + c * 128: pob * 512 + (c + 1) * 128]),
                            reads=[pb[pob]], writes=[recq[rs][c]])
                    items.append(rc)

                def mu():
                    P.comp("dve", lambda e: e.tensor_tensor(
                        out=oT[osl][orow, pair, qt * 512:(qt + 1) * 512], in0=ps[orow, pob * 512:(pob + 1) * 512],
                        in1=rec[rs][srow, :], op=ALU.mult),
                        reads=[pb[pob]] + recq[rs], writes=[oTb2[osl][pair][qt][odd]])
                items.append(mu)
                return items

            from collections import deque
            urgent, backgr, normq = deque(), deque(), deque()
            wo_load(0)
            kr_unit(0)
            for t in range(16):
                v_unit(0, t)
            for qt in range(4):
                q_unit(0, qt)
                k_unit(0, qt)
            bk_pool[0] = [6, 7]
            s_op(0)
            s_op(1)
            for g in range(NIT):
                h, qt, k2 = its[g]
                if qt == 0 and k2 == 0:
                    if h + 1 < NH:
                        urgent.append(lambda h=h: kr_unit(h + 1))
                        for q2 in range(4):
                            urgent.append(lambda h=h, q2=q2: q_unit(h + 1, q2))
                            urgent.append(lambda h=h, q2=q2: k_unit(h + 1, q2))
                    if h % 4 == 0:
                        hg = h // 4
                        if hg >= 1:
                            wo_load(hg)
                        if hg + 1 < 4:
                            for t in range(16):
                                backgr.append(lambda hg=hg, t=t: v_unit(hg + 1, t))
                        if hg >= 1:
                            for t in range(16):
                                for half in range(2):
                                    backgr.append(lambda hg=hg, t=t, half=half: wo_unit(hg - 1, t, half))
                psl = e_op(g)
                if g + 2 < NIT:
                    s_op(g + 2)
                pv_op(g, psl)
                if urgent:
                    urgent.popleft()()
                elif backgr:
                    backgr.popleft()()
                if normq:
                    normq.popleft()()
                if k2 == 7:
                    normq.extend(norm_items(g))
            while normq:
                normq.popleft()()
            while urgent:
                urgent.popleft()()
            while backgr:
                backgr.popleft()()
            P.transfer(hal, hb)
            bk_pool[0] = [4, 5, 6, 7]
            ns = NormStream(3, tr_banks=((0, 1), (2, 3)), junk=[(pT[0], pTb[0]), (pT[1], pTb[1])])
            for t in range(16):
                for half in range(2):
                    wo_unit(3, t, half)
                ns.advance(t)
            ns.finish()

        stores = []

        def phase_final(q):
            ns = NormStream(None, final_q=q)
            ns.finish()

        def dump_x(q):
            for t in range(16):
                stores.append(P.dma("sp", lambda e, t=t: e.dma_start(out=yout[q, t * 128:(t + 1) * 128, :],
                                                                     in_=xv[:, t, :]), reads=xb[t]))

        ctx["phase_bufs"] = []
        for q in range(nseq):
            if q == 0 or stop_after is not None:
                load_x(q)
            if stop_after == "X":
                dump_x(q)
                continue
            if q == 0 or stop_after is not None:
                phase_norm(0, fold=True)
            if stop_after == "A0nc":
                dump_x(q)
                continue
            if q == 0:
                build_cast_queue()
                issue_casts(9)
            if stop_after == "A0":
                dump_x(q)
                continue
            ctx["q"] = q
            phase_fourier()
            if q == 0:
                while len(cast_q) > 50:
                    issue_casts(1)
            if (stop_after or "").startswith("B"):
                dump_x(q)
                continue
            phase_ffn(0, 2)
            issue_casts(len(cast_q))
            if stop_after == "C0":
                dump_x(q)
                continue
            phase_mla()
            if stop_after == "D":
                dump_x(q)
                continue
            if stop_after == "C1":
                phase_ffn(1, 0)
                dump_x(q)
                continue
            phase_ffn(1, "final")
        P.comp("sp", None, extra=stores)
        P.finalize()

    from contextlib import ExitStack
    with ExitStack() as es:
        arena = es.enter_context(nc.sbuf_tensor("arena", [128, TOTAL // 2], BF16))
        ps = es.enter_context(nc.psum_tensor("ps", [128, 4096], F32))
        ctx["arena"] = arena
        ctx["ps"] = ps
        eng_sems = {}
        for e in Prog.ENGS:
            eng_sems[e] = es.enter_context(nc.semaphore("s_" + e))
        dma_sems = [es.enter_context(nc.semaphore("d%d" % i)) for i in range(NDMASEM)]
        rec_all()
        block = es.enter_context(nc.Block())

        @block.tensor
        def _(e):
            P.emit("pe", e, eng_sems, dma_sems)

        @block.scalar
        def _(e):
            P.emit("act", e, eng_sems, dma_sems)

        @block.vector
        def _(e):
            P.emit("dve", e, eng_sems, dma_sems)

        @block.gpsimd
        def _(e):
            P.emit("pool", e, eng_sems, dma_sems)

        @block.sync
        def _(e):
            P.emit("sp", e, eng_sems, dma_sems)
    return nc


_CONST = {}


def _constants():
    if _CONST:
        return _CONST
    bf = ml_dtypes.bfloat16
    s = np.arange(S, dtype=np.int64)
    prod = (s[:, None] * s[None, :]) % S
    ang = prod.astype(np.float64) * (2.0 * np.pi / S)
    cs = (np.cos(ang) / np.sqrt(S))
    sn = (-np.sin(ang) / np.sqrt(S))

    def lay(m):
        return np.ascontiguousarray(m.reshape(16, 128, 16, 128).transpose(2, 1, 0, 3)).reshape(16, 128, 2048).astype(bf)
    _CONST["dftc"] = lay(cs)
    _CONST["dfts"] = lay(sn)
    c = np.arange(256, dtype=np.int64)
    pc = (c[:, None] * c[None, :]) % 256
    ac = pc.astype(np.float64) * (2.0 * np.pi / 256)
    cc = np.cos(ac) / 16.0
    sc = np.sin(ac) / 16.0
    chm = np.zeros((128, 3, 2, 256), np.float64)
    for m, mat in enumerate((cc, sc, -sc)):
        chm[:, m] = mat.reshape(2, 128, 256).transpose(1, 0, 2)
    _CONST["chm"] = chm.reshape(128, 1536).astype(bf)
    _CONST["ident"] = np.eye(128, dtype=np.float32)
    _CONST["jrev"] = np.ascontiguousarray(np.eye(128, dtype=np.float32)[::-1])
    _CONST["alt"] = (((-1.0) ** np.arange(128)) / np.sqrt(S)).reshape(1, 128).astype(bf)
    inv = (1.0 / (np.float32(10000.0) ** (np.arange(0, 32, 2, dtype=np.float32) / np.float32(32)))).astype(np.float32)
    angr = (np.arange(S, dtype=np.float32)[:, None] * inv[None, :]).astype(np.float32)
    cosr = np.cos(angr).astype(np.float32).T
    sinr = np.sin(angr).astype(np.float32).T
    rope = np.ones((128, S), np.float32)
    rope[64:80] = cosr
    rope[80:96] = cosr
    rope[96:112] = -sinr
    rope[112:128] = sinr
    _CONST["rope"] = rope
    return _CONST


_NC_CACHE = {}


def _prep_weights(inp):
    f = np.float32
    cst = np.zeros((128, NCST), f)
    for n, vec in enumerate([inp["norm_mix"][0], inp["norm_ffn"][0], inp["norm_mix"][1], inp["norm_ffn"][1]]):
        cst[:, n * 8:(n + 1) * 8] = np.asarray(vec, f).reshape(8, 128).T
    cst[:, 32:34] = np.asarray(inp["g_mla_q"][0], f).reshape(2, 128).T
    cst[:, 34] = np.asarray(inp["g_mla_kv"][0], f)
    cw = np.asarray(inp["conv_w"], f)
    cb = np.asarray(inp["conv_b"], f)
    conv = np.zeros((128, 2, 44, 4), f)
    for l in range(2):
        for k in range(3):
            conv[:, l, :, k] = cw[l, k].reshape(44, 128).T
        conv[:, l, :, 3] = cb[l].reshape(44, 128).T
    cst[:, CST_CONV:CST_CONV + 352] = conv.reshape(128, 352)
    cst[:, CST_GF:CST_GF + 1024] = np.asarray(inp["norm_final"], f)[None, :]
    w_in = np.asarray(inp["w_mla_in"][0], f)
    swap = np.array(SWAP)
    win = np.concatenate([w_in[:, 0:384], w_in[:, 0:64], w_in[:, 384:416], w_in[:, 384 + swap]], axis=1)
    w_uq = np.asarray(inp["w_mla_uq"][0], f).reshape(256, 16, 96)
    wuqm = np.concatenate([w_uq, w_uq[:, :, 64 + swap]], axis=2).reshape(256, 2048)
    w_ukv = np.asarray(inp["w_mla_ukv"][0], f).reshape(128, 16, 128)
    wukv = np.concatenate([w_ukv[:, :, :64].reshape(128, 1024), w_ukv[:, :, 64:].reshape(128, 1024)], axis=1)
    return {
        "cst": cst,
        "wfo": np.ascontiguousarray(inp["w_fourier_out"][0], dtype=f),
        "wup": np.ascontiguousarray(inp["w_ffn_up"], dtype=f),
        "wdn": np.ascontiguousarray(inp["w_ffn_down"], dtype=f),
        "win": np.ascontiguousarray(win),
        "wuqm": np.ascontiguousarray(wuqm),
        "wukv": np.ascontiguousarray(wukv),
        "wo": np.ascontiguousarray(inp["w_mla_o"][0], dtype=f),
    }


def kernel(**inputs):
    inp = {k: np.asarray(v) for k, v in inputs.items()}
    xp = np.asarray(inp["x_prompt"], np.float32)
    xs = np.asarray(inp["x_sample"], np.float32)
    base = dict(_constants())
    base.update(_prep_weights(inp))
    if "full" not in _NC_CACHE:
        _NC_CACHE["full"] = build_nc(NSEQ, None)
    nc = _NC_CACHE["full"]
    in_maps = []
    for c in range(NCORES):
        m = dict(base)
        m["x"] = np.ascontiguousarray(np.concatenate([xp[4 * c:4 * c + 4], xs[c:c + 1]], axis=0))
        in_maps.append(m)
    res = run_bass_kernel_spmd(nc, in_maps, core_ids=list(range(NCORES)))
    yp = np.empty_like(xp)
    ys = np.empty_like(xs)
    for c in range(NCORES):
        y = np.asarray(res.results[c]["y"], np.float32)
        yp[4 * c:4 * c + 4] = y[0:4]
        ys[c] = y[4]
    return (yp, ys)
```

```python
import numpy as np
import ml_dtypes
import concourse.bass as bass
import concourse.mybir as mybir
from concourse.bass_utils import run_bass_kernel_spmd

F32 = mybir.dt.float32
BF16 = mybir.dt.bfloat16
AF = mybir.ActivationFunctionType
ALU = mybir.AluOpType

S = 2048
D = 1024
NT = 16
KC = 8
FF = 2816
FC = 22
NH = 16
EPS = 1e-6
ATTN_SCALE = 96.0 ** -0.5
NCORES = 8
NSEQ = 5
NCST = 1412
CST_CONV = 36
CST_GF = 388
PARTS = [list(range(0, 5)), list(range(5, 10)), list(range(10, 14)), list(range(14, 18)), list(range(18, 22))]
NDMASEM = 32
NPOOLSEM = 8
SWAP = list(range(16, 32)) + list(range(0, 16))


class Buf:
    __slots__ = ("name", "w", "r", "rd")

    def __init__(self, name=""):
        self.name = name
        self.w = None
        self.r = {}
        self.rd = []


class Op:
    __slots__ = ("eng", "fn", "deps", "sig", "cnt", "is_dma", "sem", "val", "seq")


class Prog:
    ENGS = ["pe", "act", "dve", "pool", "sp"]

    def __init__(self):
        self.ops = {e: [] for e in self.ENGS}
        self.nseq = 0
        self.ndma = {"sp": 0, "pool": 0}
        self.dma_count = [0] * NDMASEM
        self.dma_last = {}

    def op(self, eng, fn, reads=(), writes=(), extra=()):
        o = Op()
        o.eng = eng
        o.fn = fn
        o.sig = False
        o.is_dma = False
        o.cnt = 0
        o.seq = self.nseq
        self.nseq += 1
        deps = []
        seen = set()

        def add(d):
            if d is None or id(d) in seen:
                return
            seen.add(id(d))
            deps.append(d)

        for b in reads:
            add(b.w)
        for b in writes:
            add(b.w)
            for r in b.r.values():
                add(r)
            for r in b.rd:
                add(r)
        for d in extra:
            add(d)
        o.deps = [d for d in deps if d.is_dma or not (d.eng == "pe" and eng == "pe")]
        for d in o.deps:
            d.sig = True
        self.ops[eng].append(o)
        return o

    def commit(self, o, reads, writes):
        for b in reads:
            if o.is_dma:
                b.rd.append(o)
            else:
                b.r[o.eng] = o
        for b in writes:
            b.w = o
            b.r = {}
            b.rd = []

    def comp(self, eng, fn, reads=(), writes=(), extra=()):
        o = self.op(eng, fn, reads, writes, extra)
        self.commit(o, reads, writes)
        return o

    def dma(self, queue, fn, reads=(), writes=()):
        o = self.op(queue, fn, reads, writes)
        o.is_dma = True
        if queue == "pool":
            k = self.ndma["pool"] % NPOOLSEM
        else:
            k = NPOOLSEM + self.ndma["sp"] % (NDMASEM - NPOOLSEM)
        self.ndma[queue] += 1
        self.dma_count[k] += 1
        o.sem = k
        o.val = 16 * self.dma_count[k]
        prev = self.dma_last.get(k)
        if prev is not None:
            o.deps.append(prev)
        self.dma_last[k] = o
        self.commit(o, reads, writes)
        return o

    def transfer(self, src, dst):
        pend_r = {}
        pend_d = []
        ws = []
        for b in src:
            if b.w is not None:
                ws.append(b.w)
            for e, r in b.r.items():
                pend_r.setdefault(e, []).append(r)
            pend_d.extend(b.rd)
        for b in dst:
            b.w = None
            b.r = {}
            b.rd = list(pend_d)
            for w in ws:
                if w.is_dma:
                    b.rd.append(w)
            for e, lst in pend_r.items():
                b.r[e] = max(lst, key=lambda o_: o_.seq)
            for w in ws:
                if not w.is_dma:
                    cur = b.r.get(w.eng)
                    if cur is None or w.seq > cur.seq:
                        b.r[w.eng] = w

    def finalize(self):
        for e in self.ENGS:
            c = 0
            for o in self.ops[e]:
                if (not o.is_dma) and o.sig:
                    c += 1
                    o.cnt = c

    def emit(self, eng_name, eng, eng_sems, dma_sems):
        waited = {}
        for o in self.ops[eng_name]:
            need = {}
            for d in o.deps:
                if d.is_dma:
                    key = ("d", d.sem)
                    val = d.val
                else:
                    key = ("e", d.eng)
                    val = d.cnt
                if need.get(key, 0) < val:
                    need[key] = val
            for key, val in need.items():
                if waited.get(key, 0) < val:
                    sem = dma_sems[key[1]] if key[0] == "d" else eng_sems[key[1]]
                    eng.wait_ge(sem, val)
                    waited[key] = val
            if o.fn is None:
                continue
            inst = o.fn(eng)
            if o.is_dma:
                inst.then_inc(dma_sems[o.sem], 16)
            elif o.sig:
                inst.then_inc(eng_sems[eng_name], 1)


def build_nc(nseq=NSEQ, stop_after=None):
    nc = bass.Bass("TRN2", target_bir_lowering=False)
    P = Prog()

    def din(name, shape, dt=F32):
        return nc.dram_tensor(name, list(shape), dt, kind="ExternalInput").ap()

    def dscr(name, shape, dt=BF16):
        return nc.dram_tensor(name, list(shape), dt, kind="Internal").ap()

    xin = din("x", [nseq, S, D])
    yout = nc.dram_tensor("y", [nseq, S, D], F32, kind="ExternalOutput").ap()
    cst_d = din("cst", [128, NCST])
    ident_d = din("ident", [128, 128])
    rope_d = din("rope", [128, S])
    dftc_d = din("dftc", [16, 128, 2048], BF16)
    dfts_d = din("dfts", [16, 128, 2048], BF16)
    ch_d = din("chm", [128, 1536], BF16)
    jrev_d = din("jrev", [128, 128])
    alt_d = din("alt", [1, 128], BF16)
    wfo_f = din("wfo", [D, D])
    wup_f = din("wup", [2, D, 2 * FF])
    wdn_f = din("wdn", [2, FF, D])
    win_f = din("win", [D, 512])
    wuqm_f = din("wuqm", [256, 2048])
    wukv_f = din("wukv", [128, 2048])
    wo_f = din("wo", [D, D])

    wfo_s = dscr("wfo_s", [D, D])
    wup_s = dscr("wup_s", [2, 44, 128, 1024])
    wdn_s = dscr("wdn_s", [2, FF, D])
    win_s = dscr("win_s", [D, 512])
    wuqm_s = dscr("wuqm_s", [256, 2048])
    wukv_s = dscr("wukv_s", [128, 2048])
    wo_s = dscr("wo_s", [D, D])

    b_wfo_s = Buf()
    b_wup_s = [[Buf() for _ in range(44)] for _ in range(2)]
    b_wdn_s = [[Buf() for _ in range(2)] for _ in range(2)]
    b_win_s, b_wuqm_s, b_wukv_s, b_wo_s = Buf(), Buf(), Buf(), Buf()

    off = [0]

    def alloc(nbytes):
        a = off[0]
        off[0] += (nbytes + 31) // 32 * 32
        return a

    A_X = alloc(16 * 1024 * 4)
    A_HT = alloc(8 * 2048 * 2)
    A_CST = alloc(NCST * 4)
    A_ID = alloc(128 * 4)
    A_JREV = alloc(128 * 4)
    A_ALT = alloc(128 * 2)
    A_H1024 = alloc(8 * 2)
    A_SS = alloc(16 * 4)
    A_TMP = alloc(16 * 4)
    A_RSTD = alloc(16 * 4)
    A_SS2 = alloc(16 * 2 * 4)
    A_TMP2 = alloc(16 * 2 * 4)
    A_R2 = alloc(16 * 2 * 4)
    A_MH = alloc(2 * 4)
    A_STG = [alloc(1024 * 4) for _ in range(4)]
    A_PH = off[0]
    TOTAL = 207 * 1024 + 512
    PH_BYTES = TOTAL - A_PH

    ctx = {}

    def rec_all():
        arena = ctx["arena"]
        ps = ctx["ps"]

        def view(off_b, n, dt):
            if dt == BF16:
                return arena[:, off_b // 2: off_b // 2 + n]
            return arena[:, off_b // 2: off_b // 2 + 2 * n].bitcast(F32)

        xv = view(A_X, 16 * 1024, F32).rearrange("p (t d) -> p t d", d=1024)
        hv = view(A_HT, 8 * 2048, BF16).rearrange("p (k s) -> p k s", s=2048)
        cst = view(A_CST, NCST, F32)
        ident = view(A_ID, 128, F32)
        jrev = view(A_JREV, 128, F32)
        altv = view(A_ALT, 128, BF16)
        h1024 = view(A_H1024, 8, BF16)
        ss = view(A_SS, 16, F32)
        tmp = view(A_TMP, 16, F32)
        rstd = view(A_RSTD, 16, F32)
        ss2 = view(A_SS2, 32, F32)
        tmp2 = view(A_TMP2, 32, F32)
        r2 = view(A_R2, 32, F32)
        mh = view(A_MH, 2, F32)
        stg = [view(a, 1024, F32) for a in A_STG]

        xb = [[Buf(), Buf()] for _ in range(16)]
        hb = [Buf() for _ in range(16)]
        pb = [Buf() for _ in range(8)]
        cstb, identb, mhb = Buf(), Buf(), Buf()
        jrevb, altb, h1024b, g1024b = Buf(), Buf(), Buf(), Buf()
        ssb = [Buf() for _ in range(16)]
        tmpb = [Buf() for _ in range(16)]
        rstdb = [Buf() for _ in range(16)]
        ss2b = [Buf() for _ in range(16)]
        tmp2b = [Buf() for _ in range(16)]
        r2b = [Buf() for _ in range(16)]
        stgb = [Buf() for _ in range(4)]

        def bank(b, n=1):
            return ps[:, b * 512:(b + n) * 512]

        P.dma("sp", lambda e: e.dma_start(out=cst, in_=cst_d[:, :]), writes=[cstb])
        P.dma("sp", lambda e: e.dma_start(out=ident, in_=ident_d[:, :]), writes=[identb])
        P.dma("sp", lambda e: e.dma_start(out=jrev, in_=jrev_d[:, :]), writes=[jrevb])
        P.dma("sp", lambda e: e.dma_start(out=altv[0:1, :], in_=alt_d[:, :]), writes=[altb])
        P.comp("dve", lambda e: e.memset(mh, -0.5), writes=[mhb])

        cast_q = []

        def build_cast_queue():
            cast_q.append(lambda: P.dma("pool", lambda e: e.dma_start(out=wfo_s[:, :], in_=wfo_f[:, :]),
                                        writes=[b_wfo_s]))
            for l in range(2):
                if l == 1:
                    cast_q.append(lambda: P.dma("pool", lambda e: e.dma_start(out=win_s[:, :], in_=win_f[:, :]),
                                                writes=[b_win_s]))
                    cast_q.append(lambda: P.dma("pool", lambda e: e.dma_start(out=wuqm_s[:, :], in_=wuqm_f[:, :]),
                                                writes=[b_wuqm_s]))
                    cast_q.append(lambda: P.dma("pool", lambda e: e.dma_start(out=wukv_s[:, :], in_=wukv_f[:, :]),
                                                writes=[b_wukv_s]))
                    cast_q.append(lambda: P.dma("pool", lambda e: e.dma_start(out=wo_s[:, :], in_=wo_f[:, :]),
                                                writes=[b_wo_s]))
                for jj in range(22):
                    for j in (jj, 22 + jj):
                        cast_q.append(lambda l=l, j=j: P.dma("pool", lambda e: e.dma_start(
                            out=wup_s[l, j].rearrange("p (k c) -> p k c", c=128),
                            in_=wup_f[l][:, j * 128:(j + 1) * 128].rearrange("(k p) c -> p k c", p=128)),
                            writes=[b_wup_s[l][j]]))
                    if jj == 1 or jj == 8:
                        hh = 0 if jj == 1 else 1
                        cast_q.append(lambda l=l, hh=hh: P.dma("pool", lambda e: e.dma_start(
                            out=wdn_s[l][hh * 1408:(hh + 1) * 1408, :], in_=wdn_f[l][hh * 1408:(hh + 1) * 1408, :]),
                            writes=[b_wdn_s[l][hh]]))

        def issue_casts(n):
            for _ in range(n):
                if cast_q:
                    cast_q.pop(0)()

        def load_x_tile(q, t):
            P.dma("sp", lambda e: e.dma_start(out=xv[:, t, :], in_=xin[q, t * 128:(t + 1) * 128, :]), writes=xb[t])

        def load_x(q):
            for t in range(16):
                load_x_tile(q, t)

        def rstd_ops(t, src_ss, src_b):
            P.comp("pool", lambda e, t=t: e.tensor_scalar(out=tmp[:, t:t + 1], in0=src_ss[:, t:t + 1], scalar1=1.0 / D,
                                                          scalar2=EPS, op0=ALU.mult, op1=ALU.add),
                   reads=[src_b[t]], writes=[tmpb[t]])
            P.comp("pool", lambda e, t=t: e.tensor_tensor(out=rstd[:, t:t + 1], in0=tmp[:, t:t + 1], in1=mh[:, 0:1],
                                                          op=ALU.pow),
                   reads=[tmpb[t], mhb], writes=[rstdb[t]])

        class NormStream:
            def __init__(self, n, fold=False, tr_banks=((0, 1), (2, 3)), junk=None, final_q=None, chain=None,
                         xn_alt=False):
                self.n, self.fold, self.tr_banks, self.junk, self.final_q = n, fold, tr_banks, junk, final_q
                self.chain = chain
                self.xn_alt = xn_alt
                self.LA = 2
                self.ready = -1
                self.done_rest = -1
                if fold:
                    P.comp("dve", lambda e: e.memset(hv[:, :, 1024:1025], 0.0), writes=[hb[8]])

            def sq(self, t):
                if self.junk is None:
                    gb = 4 + 2 * (t % 2)
                    out_ap, wb = bank(gb, 2), [pb[gb], pb[gb + 1]]
                else:
                    ja, jb = self.junk[t % len(self.junk)]
                    out_ap, wb = ja, [jb]
                P.comp("act", lambda e: e.activation(out=out_ap, in_=xv[:, t, :], func=AF.Square,
                                                     accum_out=ss[:, t:t + 1]),
                       reads=xb[t], writes=[ssb[t]] + wb)

            def rest(self, t):
                sl = t % 4
                if self.final_q is not None:
                    q = self.final_q
                    P.comp("dve", lambda e: e.scalar_tensor_tensor(
                        out=stg[sl], in0=xv[:, t, :], scalar=rstd[:, t:t + 1], in1=cst[:, CST_GF:CST_GF + 1024],
                        op0=ALU.mult, op1=ALU.mult), reads=xb[t] + [rstdb[t], cstb], writes=[stgb[sl]])
                    stores.append(P.dma("sp", lambda e: e.dma_start(out=yout[q, t * 128:(t + 1) * 128, :],
                                                                    in_=stg[sl]), reads=[stgb[sl]]))
                    if q + 1 < nseq:
                        load_x_tile(q + 1, t)
                    return
                n, fold = self.n, self.fold
                if self.xn_alt and t % 2 == 0:
                    P.comp("dve", lambda e: e.tensor_scalar(out=stg[sl], in0=xv[:, t, :], scalar1=rstd[:, t:t + 1],
                                                            scalar2=None, op0=ALU.mult),
                           reads=xb[t] + [rstdb[t]], writes=[stgb[sl]])
                else:
                    P.comp("pool", lambda e: e.tensor_scalar(out=stg[sl], in0=xv[:, t, :],
                                                             scalar1=rstd[:, t:t + 1], scalar2=1.0,
                                                             op0=ALU.mult, op1=ALU.mult),
                           reads=xb[t] + [rstdb[t]], writes=[stgb[sl]])
                b0, b1 = self.tr_banks[t % len(self.tr_banks)]
                assert b1 == b0 + 1
                rev = fold and t >= 8

                def tr(e):
                    for kc in range(8):
                        o_ = ps[:, b0 * 512 + kc * 128: b0 * 512 + (kc + 1) * 128]
                        if rev:
                            i = e.matmul(o_, lhsT=stg[sl][:, kc * 128:(kc + 1) * 128], rhs=jrev, start=True, stop=True)
                        else:
                            i = e.transpose(o_, stg[sl][:, kc * 128:(kc + 1) * 128], ident)
                    return i
                P.comp("pe", tr, reads=[stgb[sl], identb, jrevb], writes=[pb[b0], pb[b0 + 1]])
                psT = bank(b0, 2).rearrange("p (a b) -> p a b", b=128)
                gsl = cst[:, n * 8:(n + 1) * 8]
                if not rev:
                    P.comp("dve", lambda e: e.tensor_tensor(
                        out=hv[:, :, t * 128:(t + 1) * 128], in0=psT,
                        in1=gsl.unsqueeze(2).to_broadcast([128, 8, 128]), op=ALU.mult),
                        reads=[pb[b0], pb[b0 + 1], cstb], writes=[hb[t]])
                elif t >= 9:
                    a_ = 3072 - 128 * t - 127
                    P.comp("dve", lambda e: e.tensor_tensor(
                        out=hv[:, :, a_:a_ + 128], in0=psT,
                        in1=gsl.unsqueeze(2).to_broadcast([128, 8, 128]), op=ALU.mult),
                        reads=[pb[b0], pb[b0 + 1], cstb], writes=[hb[a_ // 128], hb[(a_ + 127) // 128]])
                else:
                    P.comp("dve", lambda e: e.tensor_tensor(
                        out=hv[:, :, 1921:2048], in0=psT[:, :, 0:127],
                        in1=gsl.unsqueeze(2).to_broadcast([128, 8, 127]), op=ALU.mult),
                        reads=[pb[b0], pb[b0 + 1], cstb], writes=[hb[15]])
                    P.comp("dve", lambda e: e.tensor_tensor(
                        out=h1024.unsqueeze(2), in0=psT[:, :, 127:128],
                        in1=gsl.unsqueeze(2), op=ALU.mult),
                        reads=[pb[b0], pb[b0 + 1], cstb], writes=[h1024b])

            def advance(self, t):
                while self.ready < t:
                    self.ready += 1
                    r = self.ready - self.LA
                    if r >= 0:
                        self.rest(r)
                        self.done_rest = r
                    if self.ready - 1 >= 0:
                        rstd_ops(self.ready - 1, ss, ssb)
                    self.sq(self.ready)

            def finish(self):
                self.advance(15)
                rstd_ops(15, ss, ssb)
                while self.done_rest < 15:
                    self.done_rest += 1
                    self.rest(self.done_rest)
                if self.chain is not None:
                    self.chain.finish()

        def phase_norm(n, fold=False):
            ns = NormStream(n, fold, xn_alt=True)
            ns.finish()

        def phase_fourier():
            o = A_PH
            Ec = view(o, 8 * 1024, BF16).rearrange("p (t d) -> p t d", d=1024); o += 16384
            Es = view(o, 8 * 1024, BF16).rearrange("p (t d) -> p t d", d=1024); o += 16384
            chv = view(o, 1536, BF16).rearrange("p (m k c) -> p m k c", m=3, k=2); o += 3072
            dcv, dsv = [], []
            for _ in range(2):
                dcv.append(view(o, 1024, BF16).rearrange("p (k m) -> p k m", m=128)); o += 2048
                dsv.append(view(o, 1024, BF16).rearrange("p (k m) -> p k m", m=128)); o += 2048
            ftok = [stg[0], stg[1]]
            fT = []
            for _ in range(2):
                fT.append(view(o, 1024, BF16).rearrange("p (k m) -> p k m", m=128)); o += 2048
            g1024 = view(o, 1024, BF16); o += 2048
            wfo = view(o, 8 * 1024, BF16).rearrange("p (k n) -> p k n", n=1024); o += 16384
            fjunk = []
            for _ in range(2):
                fjunk.append((view(o, 1024, BF16), Buf())); o += 2048
            assert o - A_PH <= PH_BYTES, (o - A_PH, PH_BYTES)
            Ecb = [Buf() for _ in range(8)]
            Esb = [Buf() for _ in range(8)]
            chb, wfob = Buf(), Buf()
            dcb = [Buf(), Buf()]
            dsb = [Buf(), Buf()]
            ftokb = [stgb[0], stgb[1]]
            fTb = [Buf(), Buf()]
            allph = Ecb + Esb + [chb] + dcb + dsb + fTb + [g1024b, wfob] + [jb for _, jb in fjunk]
            P.transfer(ctx["phase_bufs"], allph)
            ctx["phase_bufs"] = allph

            P.dma("sp", lambda e: e.dma_start(out=chv.rearrange("p m k c -> p (m k c)"), in_=ch_d[:, :]), writes=[chb])

            def load_dft(j):
                sl = j % 2
                P.dma("sp", lambda e: e.dma_start(out=dcv[sl].rearrange("p k m -> p (k m)"), in_=dftc_d[j][:, 0:1024]),
                      writes=[dcb[sl]])
                P.dma("sp", lambda e: e.dma_start(out=dsv[sl].rearrange("p k m -> p (k m)"), in_=dfts_d[j][:, 0:1024]),
                      writes=[dsb[sl]])
            load_dft(0)
            load_dft(1)
            for k in range(8):
                b = 4 * (k % 2)

                def mm(e, k=k, b=b):
                    for g in range(4):
                        oc = ps[:, b * 512 + g * 256: b * 512 + (g + 1) * 256]
                        os_ = ps[:, (b + 2) * 512 + g * 256: (b + 2) * 512 + (g + 1) * 256]
                        n_ = 0
                        for mir in range(2):
                            c0_ = 128 * k if mir == 0 else 1024 + 128 * k
                            for kl in range(2):
                                kc = 2 * g + kl
                                lhsT = hv[:, kc, c0_:c0_ + 128]
                                e.matmul(oc, lhsT=lhsT, rhs=chv[:, 0, kl, :], start=(n_ == 0), stop=(n_ == 3))
                                i = e.matmul(os_, lhsT=lhsT, rhs=chv[:, 1 + mir, kl, :], start=(n_ == 0), stop=(n_ == 3))
                                n_ += 1
                    return i
                P.comp("pe", mm, reads=[hb[k], hb[8 + k], chb], writes=pb[b:b + 4])
                P.comp("act", lambda e, k=k, b=b: e.activation(out=Ec[:, k, :], in_=bank(b, 2), func=AF.Copy),
                       reads=pb[b:b + 2], writes=[Ecb[k]])
                P.comp("dve", lambda e, k=k, b=b: e.tensor_copy(out=Es[:, k, :], in_=bank(b + 2, 2)),
                       reads=pb[b + 2:b + 4], writes=[Esb[k]])

            def mm1024(e):
                for g in range(4):
                    for kl in range(2):
                        kc = 2 * g + kl
                        i = e.matmul(ps[0:1, g * 256:(g + 1) * 256], lhsT=h1024[:, kc:kc + 1], rhs=chv[:, 0, kl, :],
                                     start=(kl == 0), stop=(kl == 1))
                return i
            P.comp("pe", mm1024, reads=[h1024b, chb], writes=pb[0:2])
            P.comp("dve", lambda e: e.tensor_copy(out=g1024[0:1, :], in_=ps[0:1, 0:1024]), reads=pb[0:2], writes=[g1024b])
            if stop_after == "B1":
                return
            P.dma("sp", lambda e: e.dma_start(out=wfo, in_=wfo_s.rearrange("(k p) n -> p k n", p=128)),
                  reads=[b_wfo_s], writes=[wfob])

            def f_half(j, half):
                sl = j % 2
                fb = 2 * (j % 2)

                def mm(e):
                    o_ = ps[:, (fb + half) * 512:(fb + half + 1) * 512]
                    for k in range(8):
                        e.matmul(o_, lhsT=dcv[sl][:, k, :], rhs=Ec[:, k, half * 512:(half + 1) * 512],
                                 start=(k == 0), stop=False)
                    for k in range(8):
                        e.matmul(o_, lhsT=dsv[sl][:, k, :], rhs=Es[:, k, half * 512:(half + 1) * 512],
                                 start=False, stop=False)
                    return e.matmul(o_, lhsT=altv[0:1, :], rhs=g1024[0:1, half * 512:(half + 1) * 512],
                                    start=False, stop=True)
                P.comp("pe", mm, reads=[dcb[sl], dsb[sl], altb, g1024b] + Ecb + Esb, writes=[pb[fb + half]])
                if half == 1:
                    if j + 2 < 16:
                        load_dft(j + 2)
                    P.comp("act", lambda e: e.activation(out=ftok[sl], in_=bank(fb, 2), func=AF.Copy),
                           reads=pb[fb:fb + 2], writes=[ftokb[sl]])

            def t_op(j):
                sl = j % 2

                def tr(e):
                    for kc in range(8):
                        i = e.transpose(ps[:, 4 * 512 + kc * 128: 4 * 512 + (kc + 1) * 128],
                                        ftok[sl][:, kc * 128:(kc + 1) * 128], ident)
                    return i
                P.comp("pe", tr, reads=[ftokb[sl], identb], writes=pb[4:6])
                P.comp("dve", lambda e: e.tensor_copy(out=fT[sl], in_=bank(4, 2).rearrange("p (a b) -> p a b", b=128)),
                       reads=pb[4:6], writes=[fTb[sl]])

            def y_op(j):
                sl = j % 2

                def mm(e):
                    for half in range(2):
                        for kc in range(8):
                            i = e.matmul(bank(6 + half), lhsT=fT[sl][:, kc, :],
                                         rhs=wfo[:, kc, half * 512:(half + 1) * 512], start=(kc == 0), stop=(kc == 7))
                    return i
                P.comp("pe", mm, reads=[fTb[sl], wfob], writes=pb[6:8])
                P.comp("dve", lambda e: e.tensor_tensor(out=xv[:, j, :], in0=bank(6, 2), in1=xv[:, j, :], op=ALU.add),
                       reads=pb[6:8] + xb[j], writes=xb[j])
            ns = NormStream(1, tr_banks=((4, 5),), junk=fjunk)
            f_half(0, 0)
            f_half(0, 1)
            for j in range(16):
                if j + 1 < 16:
                    f_half(j + 1, 0)
                t_op(j)
                if j + 1 < 16:
                    f_half(j + 1, 1)
                y_op(j)
                ns.advance(j)
                issue_casts(3)
            ns.finish()

        def phase_ffn(l, next_norm):
            o = A_PH
            gp = []
            for _ in range(2):
                gp.append(view(o, 5 * 2048, BF16).rearrange("p (j s) -> p j s", s=2048)); o += 5 * 4096
            wdn = []
            for _ in range(2):
                wdn.append(view(o, 5 * 1024, BF16).rearrange("p (j n) -> p j n", n=1024)); o += 5 * 2048
            wup = []
            for _ in range(3):
                wup.append(view(o, 2048, BF16).rearrange("p (g k c) -> p g k c", g=2, k=8)); o += 4096
            accG, accV = [], []
            for _ in range(2):
                accG.append(view(o, 1024, F32)); o += 4096
                accV.append(view(o, 1024, F32)); o += 4096
            assert o - A_PH <= PH_BYTES, (o - A_PH, PH_BYTES)
            gpb = [[[Buf(), Buf()] for _ in range(5)] for _ in range(2)]
            wdnb = [Buf(), Buf()]
            wupb = [[Buf(), Buf()] for _ in range(3)]
            accGb = [Buf(), Buf()]
            accVb = [Buf(), Buf()]
            allph = [b for s_ in gpb for jj in s_ for b in jj] + wdnb + [b for s_ in wupb for b in s_] + accGb + accVb
            P.transfer(ctx["phase_bufs"], allph)
            ctx["phase_bufs"] = allph

            def cp(j, k):
                c = CST_CONV + (l * 44 + j) * 4 + k
                return cst[:, c:c + 1]

            chunks = [(p, jj, j) for p, part in enumerate(PARTS) for jj, j in enumerate(part)]

            def load_wup(ci):
                if ci >= len(chunks):
                    return
                _, _, j = chunks[ci]
                slot = ci % 3
                P.dma("sp", lambda e: e.dma_start(out=wup[slot][:, 0].rearrange("p k c -> p (k c)"), in_=wup_s[l, j]),
                      reads=[b_wup_s[l][j]], writes=[wupb[slot][0]])
                P.dma("sp", lambda e: e.dma_start(out=wup[slot][:, 1].rearrange("p k c -> p (k c)"),
                                                  in_=wup_s[l, 22 + j]),
                      reads=[b_wup_s[l][22 + j]], writes=[wupb[slot][1]])

            def load_wdn(p):
                part = PARTS[p]
                j0, n = part[0], len(part)
                slot = p % 2
                P.dma("sp", lambda e: e.dma_start(
                    out=wdn[slot][:, 0:n, :],
                    in_=wdn_s[l][j0 * 128:(j0 + n) * 128, :].rearrange("(f p) n -> p f n", p=128)),
                    reads=b_wdn_s[l], writes=[wdnb[slot]])

            def conv_evac(src_base, acc, accb, j_cst, hf):
                pbs = pb[src_base:src_base + 4]
                U = ps[:, src_base * 512:(src_base + 4) * 512]
                P.comp("act", lambda e: e.activation(out=acc, in_=U[:, hf * 1024:(hf + 1) * 1024], func=AF.Identity,
                                                     scale=cp(j_cst, 1), bias=cp(j_cst, 3)),
                       reads=pbs[2 * hf:2 * hf + 2] + [cstb], writes=[accb])

            def conv_taps(src_base, acc, accb, j_cst, hf):
                pbs = pb[src_base:src_base + 4]
                U = ps[:, src_base * 512:(src_base + 4) * 512]
                if hf == 0:
                    o0, i0 = acc[:, 1:1024], U[:, 0:1023]
                    o2, i2 = acc[:, 0:1024], U[:, 1:1025]
                else:
                    o0, i0 = acc[:, 0:1024], U[:, 1023:2047]
                    o2, i2 = acc[:, 0:1023], U[:, 1025:2048]
                P.comp("dve", lambda e: e.scalar_tensor_tensor(out=o0, in0=i0, scalar=cp(j_cst, 0), in1=o0,
                                                               op0=ALU.mult, op1=ALU.add),
                       reads=pbs + [accb, cstb], writes=[accb])
                P.comp("dve", lambda e: e.scalar_tensor_tensor(out=o2, in0=i2, scalar=cp(j_cst, 2), in1=o2,
                                                               op0=ALU.mult, op1=ALU.add),
                       reads=pbs + [accb, cstb], writes=[accb])

            def up_chunk(ci):
                p, jj, j = chunks[ci]
                slot = ci % 3
                gs = p % 2
                issue_casts(3)
                load_wup(ci + 2)
                for gv in range(2):
                    base = 4 * gv

                    def mm(e, gv=gv, base=base):
                        for kc in range(8):
                            for qt in range(4):
                                i = e.matmul(bank(base + qt), lhsT=wup[slot][:, gv, kc, :],
                                             rhs=hv[:, kc, qt * 512:(qt + 1) * 512], start=(kc == 0), stop=(kc == 7))
                        return i
                    P.comp("pe", mm, reads=[wupb[slot][gv]] + hb, writes=pb[base:base + 4])
                for hf in range(2):
                    conv_evac(0, accG[hf], accGb[hf], j, hf)
                for hf in range(2):
                    conv_taps(0, accG[hf], accGb[hf], j, hf)
                for hf in range(2):
                    conv_evac(4, accV[hf], accVb[hf], 22 + j, hf)
                for hf in range(2):
                    conv_taps(4, accV[hf], accVb[hf], 22 + j, hf)
                for hf in range(2):
                    P.comp("act", lambda e, hf=hf: e.activation(out=accG[hf], in_=accG[hf], func=AF.Silu),
                           reads=[accGb[hf]], writes=[accGb[hf]])
                for hf in range(2):
                    P.comp("pool", lambda e, hf=hf: e.tensor_tensor(out=gp[gs][:, jj, hf * 1024:(hf + 1) * 1024],
                                                                    in0=accG[hf], in1=accV[hf], op=ALU.mult),
                           reads=[accGb[hf], accVb[hf]], writes=[gpb[gs][jj][hf]])

            cnt = [0]

            def down_part(p):
                n = len(PARTS[p])
                gs = p % 2
                slot = p % 2
                last = (p == len(PARTS) - 1)
                ns = None
                if last:
                    junk = [(accG[0], accGb[0]), (accG[1], accGb[1])]
                    if next_norm == "final":
                        nxt = None
                        if ctx["q"] + 1 < nseq:
                            nxt = NormStream(0, fold=True, tr_banks=((4, 5), (6, 7)),
                                             junk=[(accV[0], accVb[0]), (accV[1], accVb[1])], xn_alt=True)
                        ns = NormStream(None, final_q=ctx["q"], junk=junk, chain=nxt)
                    else:
                        ns = NormStream(next_norm, tr_banks=((4, 5), (6, 7)), junk=junk)
                for t in range(16):
                    for half in range(2):
                        bk = cnt[0] % (4 if last else 8)
                        cnt[0] += 1

                        def mm(e, t=t, half=half, bk=bk):
                            for jj in range(n):
                                i = e.matmul(bank(bk), lhsT=gp[gs][:, jj, t * 128:(t + 1) * 128],
                                             rhs=wdn[slot][:, jj, half * 512:(half + 1) * 512],
                                             start=(jj == 0), stop=(jj == n - 1))
                            return i
                        P.comp("pe", mm, reads=[gpb[gs][jj][t // 8] for jj in range(n)] + [wdnb[slot]],
                               writes=[pb[bk]])
                        P.comp("dve", lambda e, t=t, half=half, bk=bk: e.tensor_tensor(
                            out=xv[:, t, half * 512:(half + 1) * 512], in0=bank(bk),
                            in1=xv[:, t, half * 512:(half + 1) * 512], op=ALU.add),
                            reads=[pb[bk], xb[t][half]], writes=[xb[t][half]])
                    if ns is not None:
                        ns.advance(t)
                if ns is not None:
                    ns.finish()

            load_wup(0)
            load_wup(1)
            load_wdn(0)
            ci = 0
            for p in range(len(PARTS)):
                if p + 1 < len(PARTS):
                    load_wdn(p + 1)
                start = 1 if p > 0 else 0
                for jj in range(start, len(PARTS[p])):
                    up_chunk(ci)
                    ci += 1
                if p + 1 < len(PARTS):
                    up_chunk(ci)
                    ci += 1
                down_part(p)

        def phase_mla():
            o = A_PH
            cqT = view(o, 2 * 2048, BF16).rearrange("p (k s) -> p k s", s=2048); o += 8192
            ckvT = view(o, 2048, BF16); o += 4096
            krT = view(o, 2048, BF16); o += 4096
            rope = view(o, 2048, F32); o += 8192
            t12 = []
            for _ in range(2):
                t12.append(view(o, 512, F32)); o += 2048
            wuqm = view(o, 2 * 2048, BF16).rearrange("p (k n) -> p k n", n=2048); o += 8192
            wukv = view(o, 2048, BF16); o += 4096
            oT = []
            for _ in range(2):
                oT.append(view(o, 2 * 2048, BF16).rearrange("p (c s) -> p c s", s=2048)); o += 8192
            o_d = o
            win = view(o, 8 * 512, BF16).rearrange("p (k n) -> p k n", n=512); o += 8192
            an = []
            for _ in range(4):
                an.append(view(o, 384, F32)); o += 1536
            o1 = o
            o = o_d
            pT = []
            for _ in range(3):
                pT.append(view(o, 1024, BF16)); o += 2048
            rec = []
            for _ in range(2):
                rec.append(view(o, 512, F32)); o += 2048
            wo = []
            for _ in range(2):
                wo.append(view(o, 2 * 1024, BF16).rearrange("p (c n) -> p c n", n=1024)); o += 4096
            vaug1 = view(o, 16 * 384, BF16).rearrange("p (t c) -> p t c", c=384); o += 12288
            qscr2 = view(o, 512, F32); o += 2048
            assert max(o, o1) - A_PH <= PH_BYTES, (max(o, o1) - A_PH, PH_BYTES)
            oh = A_HT
            Qh, Kh = [], []
            for _ in range(2):
                Qh.append(view(oh, 2048, BF16)); oh += 4096
                Kh.append(view(oh, 2048, BF16)); oh += 4096
            vaug = view(oh, 16 * 384, BF16).rearrange("p (t c) -> p t c", c=384); oh += 12288
            assert oh - A_HT <= 32768
            vaug2 = [vaug, vaug1]

            cqTb = [Buf() for _ in range(16)]
            ckvTb = [Buf() for _ in range(16)]
            krTb = [Buf() for _ in range(4)]
            ropeb, wuqmb, wukvb, winb = Buf(), Buf(), Buf(), Buf()
            t12b = [Buf(), Buf()]
            oTb = [[[Buf() for _ in range(4)] for _ in range(2)] for _ in range(2)]
            oTb2 = [[[[Buf() for _ in range(2)] for _ in range(4)] for _ in range(2)] for _ in range(2)]
            anb = [Buf() for _ in range(4)]
            pTb = [Buf(), Buf(), Buf()]
            recb = [Buf(), Buf()]
            recq = [[Buf() for _ in range(4)] for _ in range(2)]
            t12h = [[Buf(), Buf()], [Buf(), Buf()]]
            wob = [Buf(), Buf()]
            Qhb = [[Buf() for _ in range(4)] for _ in range(2)]
            Khb = [[Buf() for _ in range(4)] for _ in range(2)]
            Khrb = [Buf(), Buf()]
            vaugb = [[Buf() for _ in range(16)] for _ in range(2)]
            onesb = [Buf(), Buf()]
            d1 = [winb] + anb
            rest = (cqTb + ckvTb + krTb + [ropeb, wuqmb, wukvb] + t12b +
                    [b for a in oTb2 for c in a for d_ in c for b in d_])
            P.transfer(ctx["phase_bufs"], d1 + rest)
            d2 = pTb + recb + wob + vaugb[1] + [onesb[1]] + [b for r_ in recq for b in r_]
            ctx["phase_bufs"] = rest + d2

            P.dma("sp", lambda e: e.dma_start(out=win, in_=win_s.rearrange("(k p) n -> p k n", p=128)),
                  reads=[b_win_s], writes=[winb])
            P.dma("sp", lambda e: e.dma_start(out=rope, in_=rope_d[:, :]), writes=[ropeb])
            P.dma("sp", lambda e: e.dma_start(out=wuqm, in_=wuqm_s.rearrange("(k p) n -> p k n", p=128)),
                  reads=[b_wuqm_s], writes=[wuqmb])
            P.dma("sp", lambda e: e.dma_start(out=wukv, in_=wukv_s[:, :]), reads=[b_wukv_s], writes=[wukvb])

            def d1_a(t):
                bk = t % 4
                sl = t % 4

                def mm(e):
                    for kc in range(8):
                        i = e.matmul(ps[:, bk * 512: bk * 512 + 384], lhsT=hv[:, kc, t * 128:(t + 1) * 128],
                                     rhs=win[:, kc, 0:384], start=(kc == 0), stop=(kc == 7))
                    return i
                P.comp("pe", mm, reads=[hb[t], winb], writes=[pb[bk]])
                P.comp("act", lambda e: e.activation(
                    out=an[sl][:, 0:256], in_=ps[:, bk * 512: bk * 512 + 256],
                    func=AF.Square, accum_out=ss2[:, 2 * t:2 * t + 1]),
                    reads=[pb[bk]], writes=[ss2b[t], anb[sl]])
                P.comp("act", lambda e: e.activation(
                    out=an[sl][:, 256:384], in_=ps[:, bk * 512 + 256: bk * 512 + 384],
                    func=AF.Square, accum_out=ss2[:, 2 * t + 1:2 * t + 2]),
                    reads=[pb[bk], ss2b[t], anb[sl]], writes=[ss2b[t], anb[sl]])
                P.comp("pool", lambda e: e.tensor_scalar(out=tmp2[:, 2 * t:2 * t + 1], in0=ss2[:, 2 * t:2 * t + 1],
                                                         scalar1=1.0 / 256, scalar2=EPS, op0=ALU.mult, op1=ALU.add),
                       reads=[ss2b[t]], writes=[tmp2b[t]])
                P.comp("pool", lambda e: e.tensor_scalar(out=tmp2[:, 2 * t + 1:2 * t + 2],
                                                         in0=ss2[:, 2 * t + 1:2 * t + 2],
                                                         scalar1=1.0 / 128, scalar2=EPS, op0=ALU.mult, op1=ALU.add),
                       reads=[ss2b[t], tmp2b[t]], writes=[tmp2b[t]])
                P.comp("pool", lambda e: e.tensor_tensor(out=r2[:, 2 * t:2 * t + 2], in0=tmp2[:, 2 * t:2 * t + 2],
                                                         in1=mh[:, 0:2], op=ALU.pow),
                       reads=[tmp2b[t], mhb], writes=[r2b[t]])

            def d1_b(t):
                bk = t % 4
                sl = t % 4
                b2 = 4 + (t % 2)
                P.comp("dve", lambda e: e.tensor_scalar(
                    out=an[sl][:, 0:256], in0=ps[:, bk * 512: bk * 512 + 256], scalar1=r2[:, 2 * t:2 * t + 1],
                    scalar2=None, op0=ALU.mult), reads=[pb[bk], r2b[t]], writes=[anb[sl]])
                P.comp("dve", lambda e: e.tensor_scalar(
                    out=an[sl][:, 256:384], in0=ps[:, bk * 512 + 256: bk * 512 + 384],
                    scalar1=r2[:, 2 * t + 1:2 * t + 2], scalar2=None, op0=ALU.mult),
                    reads=[pb[bk], r2b[t], anb[sl]], writes=[anb[sl]])

                def tr(e):
                    for c in range(3):
                        i = e.transpose(ps[:, b2 * 512 + c * 128: b2 * 512 + (c + 1) * 128],
                                        an[sl][:, c * 128:(c + 1) * 128], ident)
                    return i
                P.comp("pe", tr, reads=[anb[sl], identb], writes=[pb[b2]])

            def d1_c(t):
                b2 = 4 + (t % 2)
                P.comp("dve", lambda e: e.tensor_tensor(
                    out=cqT[:, :, t * 128:(t + 1) * 128],
                    in0=ps[:, b2 * 512: b2 * 512 + 256].rearrange("p (a b) -> p a b", b=128),
                    in1=cst[:, 32:34].unsqueeze(2).to_broadcast([128, 2, 128]), op=ALU.mult),
                    reads=[pb[b2], cstb], writes=[cqTb[t]])
                P.comp("dve", lambda e: e.tensor_scalar(
                    out=ckvT[:, t * 128:(t + 1) * 128], in0=ps[:, b2 * 512 + 256: b2 * 512 + 384],
                    scalar1=cst[:, 34:35], scalar2=None, op0=ALU.mult),
                    reads=[pb[b2], cstb], writes=[ckvTb[t]])

            def kr_qt(qt):
                bA = 6

                def mm(e):
                    for kc in range(8):
                        i = e.matmul(ps[:, bA * 512:(bA + 1) * 512], lhsT=win[:, kc, 384:512],
                                     rhs=hv[:, kc, qt * 512:(qt + 1) * 512], start=(kc == 0), stop=(kc == 7))
                    return i
                P.comp("pe", mm, reads=[winb] + hb[4 * qt:4 * qt + 4], writes=[pb[bA]])
                P.comp("dve", lambda e: e.tensor_tensor(
                    out=t12[0][64:96, :], in0=ps[64:96, bA * 512:(bA + 1) * 512],
                    in1=rope[64:96, qt * 512:(qt + 1) * 512], op=ALU.mult),
                    reads=[pb[bA], ropeb], writes=[t12b[0]])
                P.comp("dve", lambda e: e.tensor_tensor(
                    out=t12[1][64:96, :], in0=ps[96:128, bA * 512:(bA + 1) * 512],
                    in1=rope[96:128, qt * 512:(qt + 1) * 512], op=ALU.mult),
                    reads=[pb[bA], ropeb], writes=[t12b[1]])
                P.comp("dve", lambda e: e.tensor_tensor(
                    out=krT[64:96, qt * 512:(qt + 1) * 512], in0=t12[0][64:96, :], in1=t12[1][64:96, :], op=ALU.add),
                    reads=t12b, writes=[krTb[qt]])

            for t in range(16 + 3):
                if t < 16:
                    d1_a(t)
                if 0 <= t - 2 < 16:
                    d1_b(t - 2)
                if 0 <= t - 3 < 16:
                    d1_c(t - 3)
                if t in (2, 6, 10, 14):
                    kr_qt(t // 4)

            hal = [b for s_ in Qhb for b in s_] + [b for s_ in Khb for b in s_] + Khrb + vaugb[0] + [onesb[0]]
            P.transfer(hb, hal)
            P.transfer(d1, d2)
            t12h_flat = [b for r_ in t12h for b in r_]
            P.transfer(t12b, t12h_flat)
            qscr = [t12[0], qscr2]
            qscrb = [t12h[0][0], Buf()]
            P.transfer(d1, [qscrb[1]])
            qctr = [0]
            ctx["phase_bufs"] = ctx["phase_bufs"] + t12h_flat + [qscrb[1]]
            for vs in range(2):
                vones = vaug2[vs].rearrange("p t (a b c) -> p t a b c", a=2, b=3)[:, :, :, 1, :]
                P.comp("pool", lambda e, vones=vones: e.memset(vones, 1.0), writes=[onesb[vs]])

            sbk = [0]
            bk_pool = [list(range(8))]

            def nextbk():
                sbk[0] += 1
                pool_ = bk_pool[0]
                return pool_[sbk[0] % len(pool_)]

            def kr_unit(h):
                hs = h % 2
                P.comp("pool", lambda e: e.tensor_copy(out=Kh[hs][64:96, :], in_=krT[64:96, :]),
                       reads=krTb, writes=[Khrb[hs]])

            def q_unit(h, qt):
                hs = h % 2
                bk = nextbk()
                c0_ = qt * 512
                qs = qctr[0] % 2
                qctr[0] += 1
                scr = qscr[qs]

                def mmq(e):
                    for kc in range(2):
                        i = e.matmul(bank(bk), lhsT=wuqm[:, kc, h * 128:(h + 1) * 128],
                                     rhs=cqT[:, kc, c0_:c0_ + 512], start=(kc == 0), stop=(kc == 1))
                    return i
                P.comp("pe", mmq, reads=[wuqmb] + cqTb[4 * qt:4 * qt + 4], writes=[pb[bk]])
                P.comp("dve", lambda e: e.tensor_tensor(
                    out=scr[0:96, :], in0=ps[0:96, bk * 512:(bk + 1) * 512],
                    in1=rope[0:96, c0_:c0_ + 512], op=ALU.mult),
                    reads=[pb[bk], ropeb], writes=[qscrb[qs]])
                P.comp("dve", lambda e: e.tensor_tensor(
                    out=t12[1][64:96, :], in0=ps[96:128, bk * 512:(bk + 1) * 512],
                    in1=rope[96:128, c0_:c0_ + 512], op=ALU.mult),
                    reads=[pb[bk], ropeb], writes=[t12h[1][0]])
                P.comp("pool", lambda e: e.tensor_tensor(
                    out=Qh[hs][64:96, c0_:c0_ + 512], in0=scr[64:96, :], in1=t12[1][64:96, :], op=ALU.add),
                    reads=[qscrb[qs], t12h[1][0], Qhb[hs][qt]], writes=[Qhb[hs][qt]])
                P.comp("pool", lambda e: e.tensor_copy(out=Qh[hs][0:64, c0_:c0_ + 512], in_=scr[0:64, :]),
                       reads=[qscrb[qs], Qhb[hs][qt]], writes=[Qhb[hs][qt]])

            def k_unit(h, qt):
                hs = h % 2
                bk = nextbk()
                P.comp("pe", lambda e: e.matmul(
                    ps[0:64, bk * 512:(bk + 1) * 512], lhsT=wukv[:, h * 64:(h + 1) * 64],
                    rhs=ckvT[:, qt * 512:(qt + 1) * 512], start=True, stop=True),
                    reads=[wukvb] + ckvTb[4 * qt:4 * qt + 4], writes=[pb[bk]])
                P.comp("dve", lambda e: e.tensor_copy(
                    out=Kh[hs][0:64, qt * 512:(qt + 1) * 512], in_=ps[0:64, bk * 512:(bk + 1) * 512]),
                    reads=[pb[bk]], writes=[Khb[hs][qt]])

            def v_unit(hg, t):
                vs = hg % 2
                bk = nextbk()
                P.comp("pe", lambda e: e.matmul(
                    ps[:, bk * 512: bk * 512 + 256], lhsT=ckvT[:, t * 128:(t + 1) * 128],
                    rhs=wukv[:, 1024 + hg * 256: 1024 + (hg + 1) * 256], start=True, stop=True),
                    reads=[ckvTb[t], wukvb], writes=[pb[bk]])
                P.comp("dve", lambda e: e.tensor_copy(
                    out=vaug2[vs][:, t, :].rearrange("p (a b c) -> p a b c", a=2, b=3)[:, :, 0:3:2, :],
                    in_=ps[:, bk * 512: bk * 512 + 256].rearrange("p (a b c) -> p a b c", a=2, b=2)),
                    reads=[pb[bk]], writes=[vaugb[vs][t]])

            def wo_load(hg):
                wsl = hg % 2
                P.dma("sp", lambda e: e.dma_start(
                    out=wo[wsl], in_=wo_s[hg * 256:(hg + 1) * 256, :].rearrange("(c p) n -> p c n", p=128)),
                    reads=[b_wo_s], writes=[wob[wsl]])

            def wo_unit(hg, t, half):
                osl = hg % 2
                wsl = hg % 2
                bk = nextbk()

                def mm(e):
                    for c in range(2):
                        i = e.matmul(bank(bk), lhsT=oT[osl][:, c, t * 128:(t + 1) * 128],
                                     rhs=wo[wsl][:, c, half * 512:(half + 1) * 512], start=(c == 0), stop=(c == 1))
                    return i
                P.comp("pe", mm, reads=[oTb2[osl][c][t // 4][od] for c in range(2) for od in range(2)] + [wob[wsl]],
                       writes=[pb[bk]])
                P.comp("dve", lambda e: e.tensor_tensor(
                    out=xv[:, t, half * 512:(half + 1) * 512], in0=bank(bk),
                    in1=xv[:, t, half * 512:(half + 1) * 512], op=ALU.add),
                    reads=[pb[bk], xb[t][half]], writes=[xb[t][half]])

            its = [(h, qt, k2) for h in range(NH) for qt in range(4) for k2 in range(8)]
            NIT = len(its)

            def s_op(g):
                h, qt, k2 = its[g]
                hs = h % 2
                sb = 2 * (g % 2)

                def mm(e):
                    for kk in range(2):
                        kt = 2 * k2 + kk
                        i = e.matmul(bank(sb + kk), lhsT=Kh[hs][0:96, kt * 128:(kt + 1) * 128],
                                     rhs=Qh[hs][0:96, qt * 512:(qt + 1) * 512], start=True, stop=True)
                    return i
                P.comp("pe", mm, reads=[Khb[hs][k2 // 2], Khrb[hs], Qhb[hs][qt]], writes=pb[sb:sb + 2])

            def e_op(g):
                sb = 2 * (g % 2)
                psl = g % 3
                P.comp("act", lambda e: e.activation(out=pT[psl], in_=bank(sb, 2), func=AF.Exp, scale=ATTN_SCALE),
                       reads=pb[sb:sb + 2], writes=[pTb[psl]])
                return psl

            def pv_op(g, psl):
                h, qt, k2 = its[g]
                vs = (h // 4) % 2
                hl = h % 4
                c0 = (hl // 2) * 192 + (hl % 2) * 64
                pob = 4 + ((g // 8) % 2)

                def mm(e):
                    for kk in range(2):
                        kt = 2 * k2 + kk
                        i = e.matmul(bank(pob), lhsT=vaug2[vs][:, kt, c0:c0 + 128],
                                     rhs=pT[psl][:, kk * 512:(kk + 1) * 512], start=(kt == 0), stop=(kt == 15))
                    return i
                P.comp("pe", mm, reads=[vaugb[vs][2 * k2], vaugb[vs][2 * k2 + 1], onesb[vs], pTb[psl]],
                       writes=[pb[pob]])

            def norm_items(g):
                h, qt, _ = its[g]
                hg, hl = h // 4, h % 4
                osl = hg % 2
                pair, odd = hl // 2, hl % 2
                orow = slice(64, 128) if odd else slice(0, 64)
                srow = slice(0, 64) if odd else slice(64, 128)
                pob = 4 + ((g // 8) % 2)
                rs = (g // 8) % 2
                items = []
                if False:
                    def ln_():
                        P.comp("act", lambda e: e.activation(out=rec[rs][srow, :],
                                                             in_=ps[srow, pob * 512:(pob + 1) * 512], func=AF.Ln),
                               reads=[pb[pob]], writes=recq[rs])

                    def ex_():
                        P.comp("act", lambda e: e.activation(out=rec[rs][srow, :], in_=rec[rs][srow, :],
                                                             func=AF.Exp, scale=-1.0),
                               reads=recq[rs], writes=recq[rs])
                    items.append(ln_)
                    items.append(ex_)
                for c in range(4 if not items else 0):
                    def rc(c=c):
                        P.comp("dve", lambda e: e.reciprocal(
                            out=rec[rs][srow, c * 128:(c + 1) * 128],
                            in_=ps[srow, pob * 512 + c * 128: pob * 512 + (c + 1) * 128]),
                            reads=[pb[pob]], writes=[recq[rs][c]])
                    items.append(rc)

                def mu():
                    P.comp("dve", lambda e: e.tensor_tensor(
                        out=oT[osl][orow, pair, qt * 512:(qt + 1) * 512], in0=ps[orow, pob * 512:(pob + 1) * 512],
                        in1=rec[rs][srow, :], op=ALU.mult),
                        reads=[pb[pob]] + recq[rs], writes=[oTb2[osl][pair][qt][odd]])
                items.append(mu)
                return items

            from collections import deque
            urgent, backgr, normq = deque(), deque(), deque()
            wo_load(0)
            kr_unit(0)
            for t in range(16):
                v_unit(0, t)
            for qt in range(4):
                q_unit(0, qt)
                k_unit(0, qt)
            bk_pool[0] = [6, 7]
            s_op(0)
            s_op(1)
            for g in range(NIT):
                h, qt, k2 = its[g]
                if qt == 0 and k2 == 0:
                    if h + 1 < NH:
                        urgent.append(lambda h=h: kr_unit(h + 1))
                        for q2 in range(4):
                            urgent.append(lambda h=h, q2=q2: q_unit(h + 1, q2))
                            urgent.append(lambda h=h, q2=q2: k_unit(h + 1, q2))
                    if h % 4 == 0:
                        hg = h // 4
                        if hg >= 1:
                            wo_load(hg)
                        if hg + 1 < 4:
                            for t in range(16):
                                backgr.append(lambda hg=hg, t=t: v_unit(hg + 1, t))
                        if hg >= 1:
                            for t in range(16):
                                for half in range(2):
                                    backgr.append(lambda hg=hg, t=t, half=half: wo_unit(hg - 1, t, half))
                psl = e_op(g)
                if g + 2 < NIT:
                    s_op(g + 2)
                pv_op(g, psl)
                if urgent:
                    urgent.popleft()()
                elif backgr:
                    backgr.popleft()()
                if normq:
                    normq.popleft()()
                if k2 == 7:
                    normq.extend(norm_items(g))
            while normq:
                normq.popleft()()
            while urgent:
                urgent.popleft()()
            while backgr:
                backgr.popleft()()
            P.transfer(hal, hb)
            bk_pool[0] = [4, 5, 6, 7]
            ns = NormStream(3, tr_banks=((0, 1), (2, 3)), junk=[(pT[0], pTb[0]), (pT[1], pTb[1])])
            for t in range(16):
                for half in range(2):
                    wo_unit(3, t, half)
                ns.advance(t)
            ns.finish()

        stores = []

        def phase_final(q):
            ns = NormStream(None, final_q=q)
            ns.finish()

        def dump_x(q):
            for t in range(16):
                stores.append(P.dma("sp", lambda e, t=t: e.dma_start(out=yout[q, t * 128:(t + 1) * 128, :],
                                                                     in_=xv[:, t, :]), reads=xb[t]))

        ctx["phase_bufs"] = []
        for q in range(nseq):
            if q == 0 or stop_after is not None:
                load_x(q)
            if stop_after == "X":
                dump_x(q)
                continue
            if q == 0 or stop_after is not None:
                phase_norm(0, fold=True)
            if stop_after == "A0nc":
                dump_x(q)
                continue
            if q == 0:
                build_cast_queue()
                issue_casts(9)
            if stop_after == "A0":
                dump_x(q)
                continue
            ctx["q"] = q
            phase_fourier()
            if q == 0:
                while len(cast_q) > 50:
                    issue_casts(1)
            if (stop_after or "").startswith("B"):
                dump_x(q)
                continue
            phase_ffn(0, 2)
            issue_casts(len(cast_q))
            if stop_after == "C0":
                dump_x(q)
                continue
            phase_mla()
            if stop_after == "D":
                dump_x(q)
                continue
            if stop_after == "C1":
                phase_ffn(1, 0)
                dump_x(q)
                continue
            phase_ffn(1, "final")
        P.comp("sp", None, extra=stores)
        P.finalize()

    from contextlib import ExitStack
    with ExitStack() as es:
        arena = es.enter_context(nc.sbuf_tensor("arena", [128, TOTAL // 2], BF16))
        ps = es.enter_context(nc.psum_tensor("ps", [128, 4096], F32))
        ctx["arena"] = arena
        ctx["ps"] = ps
        eng_sems = {}
        for e in Prog.ENGS:
            eng_sems[e] = es.enter_context(nc.semaphore("s_" + e))
        dma_sems = [es.enter_context(nc.semaphore("d%d" % i)) for i in range(NDMASEM)]
        rec_all()
        block = es.enter_context(nc.Block())

        @block.tensor
        def _(e):
            P.emit("pe", e, eng_sems, dma_sems)

        @block.scalar
        def _(e):
            P.emit("act", e, eng_sems, dma_sems)

        @block.vector
        def _(e):
            P.emit("dve", e, eng_sems, dma_sems)

        @block.gpsimd
        def _(e):
            P.emit("pool", e, eng_sems, dma_sems)

        @block.sync
        def _(e):
            P.emit("sp", e, eng_sems, dma_sems)
    return nc


_CONST = {}


def _constants():
    if _CONST:
        return _CONST
    bf = ml_dtypes.bfloat16
    s = np.arange(S, dtype=np.int64)
    prod = (s[:, None] * s[None, :]) % S
    ang = prod.astype(np.float64) * (2.0 * np.pi / S)
    cs = (np.cos(ang) / np.sqrt(S))
    sn = (-np.sin(ang) / np.sqrt(S))

    def lay(m):
        return np.ascontiguousarray(m.reshape(16, 128, 16, 128).transpose(2, 1, 0, 3)).reshape(16, 128, 2048).astype(bf)
    _CONST["dftc"] = lay(cs)
    _CONST["dfts"] = lay(sn)
    c = np.arange(256, dtype=np.int64)
    pc = (c[:, None] * c[None, :]) % 256
    ac = pc.astype(np.float64) * (2.0 * np.pi / 256)
    cc = np.cos(ac) / 16.0
    sc = np.sin(ac) / 16.0
    chm = np.zeros((128, 3, 2, 256), np.float64)
    for m, mat in enumerate((cc, sc, -sc)):
        chm[:, m] = mat.reshape(2, 128, 256).transpose(1, 0, 2)
    _CONST["chm"] = chm.reshape(128, 1536).astype(bf)
    _CONST["ident"] = np.eye(128, dtype=np.float32)
    _CONST["jrev"] = np.ascontiguousarray(np.eye(128, dtype=np.float32)[::-1])
    _CONST["alt"] = (((-1.0) ** np.arange(128)) / np.sqrt(S)).reshape(1, 128).astype(bf)
    inv = (1.0 / (np.float32(10000.0) ** (np.arange(0, 32, 2, dtype=np.float32) / np.float32(32)))).astype(np.float32)
    angr = (np.arange(S, dtype=np.float32)[:, None] * inv[None, :]).astype(np.float32)
    cosr = np.cos(angr).astype(np.float32).T
    sinr = np.sin(angr).astype(np.float32).T
    rope = np.ones((128, S), np.float32)
    rope[64:80] = cosr
    rope[80:96] = cosr
    rope[96:112] = -sinr
    rope[112:128] = sinr
    _CONST["rope"] = rope
    return _CONST


_NC_CACHE = {}


def _prep_weights(inp):
    f = np.float32
    cst = np.zeros((128, NCST), f)
    for n, vec in enumerate([inp["norm_mix"][0], inp["norm_ffn"][0], inp["norm_mix"][1], inp["norm_ffn"][1]]):
        cst[:, n * 8:(n + 1) * 8] = np.asarray(vec, f).reshape(8, 128).T
    cst[:, 32:34] = np.asarray(inp["g_mla_q"][0], f).reshape(2, 128).T
    cst[:, 34] = np.asarray(inp["g_mla_kv"][0], f)
    cw = np.asarray(inp["conv_w"], f)
    cb = np.asarray(inp["conv_b"], f)
    conv = np.zeros((128, 2, 44, 4), f)
    for l in range(2):
        for k in range(3):
            conv[:, l, :, k] = cw[l, k].reshape(44, 128).T
        conv[:, l, :, 3] = cb[l].reshape(44, 128).T
    cst[:, CST_CONV:CST_CONV + 352] = conv.reshape(128, 352)
    cst[:, CST_GF:CST_GF + 1024] = np.asarray(inp["norm_final"], f)[None, :]
    w_in = np.asarray(inp["w_mla_in"][0], f)
    swap = np.array(SWAP)
    win = np.concatenate([w_in[:, 0:384], w_in[:, 0:64], w_in[:, 384:416], w_in[:, 384 + swap]], axis=1)
    w_uq = np.asarray(inp["w_mla_uq"][0], f).reshape(256, 16, 96)
    wuqm = np.concatenate([w_uq, w_uq[:, :, 64 + swap]], axis=2).reshape(256, 2048)
    w_ukv = np.asarray(inp["w_mla_ukv"][0], f).reshape(128, 16, 128)
    wukv = np.concatenate([w_ukv[:, :, :64].reshape(128, 1024), w_ukv[:, :, 64:].reshape(128, 1024)], axis=1)
    return {
        "cst": cst,
        "wfo": np.ascontiguousarray(inp["w_fourier_out"][0], dtype=f),
        "wup": np.ascontiguousarray(inp["w_ffn_up"], dtype=f),
        "wdn": np.ascontiguousarray(inp["w_ffn_down"], dtype=f),
        "win": np.ascontiguousarray(win),
        "wuqm": np.ascontiguousarray(wuqm),
        "wukv": np.ascontiguousarray(wukv),
        "wo": np.ascontiguousarray(inp["w_mla_o"][0], dtype=f),
    }


def kernel(**inputs):
    inp = {k: np.asarray(v) for k, v in inputs.items()}
    xp = np.asarray(inp["x_prompt"], np.float32)
    xs = np.asarray(inp["x_sample"], np.float32)
    base = dict(_constants())
    base.update(_prep_weights(inp))
    if "full" not in _NC_CACHE:
        _NC_CACHE["full"] = build_nc(NSEQ, None)
    nc = _NC_CACHE["full"]
    in_maps = []
    for c in range(NCORES):
        m = dict(base)
        m["x"] = np.ascontiguousarray(np.concatenate([xp[4 * c:4 * c + 4], xs[c:c + 1]], axis=0))
        in_maps.append(m)
    res = run_bass_kernel_spmd(nc, in_maps, core_ids=list(range(NCORES)))
    yp = np.empty_like(xp)
    ys = np.empty_like(xs)
    for c in range(NCORES):
        y = np.asarray(res.results[c]["y"], np.float32)
        yp[4 * c:4 * c + 4] = y[0:4]
        ys[c] = y[4]
    return (yp, ys)
```
